# Optimizing a Trainium2 kernel written in Bass

```python
import math
import jax, jax.numpy as jnp
from jax import lax
import numpy as np

D_MODEL = 2048
BATCH = 4
SEQ = 4096
DEPTH = 1

HEAD_DIM = 128
A_Q_HEADS = 8
A_KV_HEADS = 2
A_GROUP = A_Q_HEADS // A_KV_HEADS
WINDOW = 128
A_BLOCK = 128
B_HEADS = 8
GRID_W = 64
WIN_H = 8
WIN_W = 16
Q_COL_BLOCK = 16
K_COL_SPAN = Q_COL_BLOCK + WIN_W
D_FF = 5632
CONV_W = 3
ROPE_THETA = 10000.0
EPS = 1e-6
NEG = -1e30

A_Q_DIM = A_Q_HEADS * HEAD_DIM
A_KV_DIM = A_KV_HEADS * HEAD_DIM
B_DIM = B_HEADS * HEAD_DIM
IN_SPLITS = [A_Q_DIM, A_KV_DIM, A_KV_DIM, B_DIM, B_DIM, B_DIM, D_MODEL, D_MODEL]
IN_COLS = sum(IN_SPLITS)

kernel_name = "hybrid_window_gqa_natten_convglu"


def rmsnorm(x, g):
    xf = x.astype(jnp.float32)
    y = xf * lax.rsqrt(jnp.mean(xf * xf, axis=-1, keepdims=True) + EPS)
    return (y * g.astype(jnp.float32)).astype(x.dtype)


def rope(x, pos):
    half = x.shape[-1] // 2
    inv_freq = ROPE_THETA ** (-jnp.arange(half, dtype=jnp.float32) * (2.0 / x.shape[-1]))
    ang = pos[:, None] * inv_freq[None, :]
    cos = jnp.cos(ang)[None, :, None, :]
    sin = jnp.sin(ang)[None, :, None, :]
    xf = x.astype(jnp.float32)
    x1, x2 = xf[..., :half], xf[..., half:]
    return jnp.concatenate([x1 * cos - x2 * sin, x2 * cos + x1 * sin], axis=-1).astype(x.dtype)


def window_attention(q, k, v, sink):
    b, s_len, _, d = q.shape
    nb = s_len // A_BLOCK
    qb = q.reshape(b, nb, A_BLOCK, A_KV_HEADS, A_GROUP, d)
    pad = ((0, 0), (A_BLOCK, A_BLOCK), (0, 0), (0, 0))
    kp = jnp.pad(k, pad).reshape(b, nb + 2, A_BLOCK, A_KV_HEADS, d)
    vp = jnp.pad(v, pad).reshape(b, nb + 2, A_BLOCK, A_KV_HEADS, d)
    kb = jnp.concatenate([kp[:, 0:nb], kp[:, 1:nb + 1], kp[:, 2:nb + 2]], axis=2)
    vb = jnp.concatenate([vp[:, 0:nb], vp[:, 1:nb + 1], vp[:, 2:nb + 2]], axis=2)
    s = jnp.einsum('bnqhgd,bnkhd->bnhgqk', qb, kb,
                   preferred_element_type=jnp.float32) * (1.0 / math.sqrt(d))
    qpos = np.arange(nb)[:, None] * A_BLOCK + np.arange(A_BLOCK)[None, :]
    kpos = (np.arange(nb)[:, None] - 1) * A_BLOCK + np.arange(3 * A_BLOCK)[None, :]
    valid = ((np.abs(qpos[:, :, None] - kpos[:, None, :]) <= WINDOW)
             & (kpos >= 0)[:, None, :] & (kpos < s_len)[:, None, :])
    s = jnp.where(jnp.asarray(valid)[None, :, None, None], s, NEG)
    sink_b = sink.astype(jnp.float32).reshape(A_KV_HEADS, A_GROUP)[None, None, :, :, None, None]
    m = jnp.maximum(jnp.max(s, axis=-1, keepdims=True), sink_b)
    p = jnp.exp(s - m)
    p = p / (jnp.sum(p, axis=-1, keepdims=True) + jnp.exp(sink_b - m))
    o = jnp.einsum('bnhgqk,bnkhd->bnqhgd', p.astype(v.dtype), vb)
    return o.reshape(b, s_len, A_Q_HEADS * d)


def neighbourhood_attention(q, k, v, rpb):
    b, s_len, h, d = q.shape
    rows = s_len // GRID_W
    kh = min(WIN_H, rows)
    rb = kh
    nrb = -(-rows // rb)
    rows_p = nrb * rb
    kspan = min(rb + kh, rows)
    ncb = GRID_W // Q_COL_BLOCK
    nq = rb * Q_COL_BLOCK
    nk = kspan * K_COL_SPAN

    qg = jnp.pad(q.reshape(b, rows, GRID_W, h, d), ((0, 0), (0, rows_p - rows), (0, 0), (0, 0), (0, 0)))
    qb = qg.reshape(b, nrb, rb, ncb, Q_COL_BLOCK, h, d).transpose(0, 1, 3, 2, 4, 5, 6)
    qb = qb.reshape(b, nrb, ncb, nq, h, d)

    r0 = np.arange(nrb) * rb
    krows = np.clip(r0 - kh // 2, 0, rows - kspan)[:, None] + np.arange(kspan)[None, :]
    c0 = np.arange(ncb) * Q_COL_BLOCK
    kcols = np.clip(c0 - WIN_W // 2, 0, GRID_W - K_COL_SPAN)[:, None] + np.arange(K_COL_SPAN)[None, :]

    kg = k.reshape(b, rows, GRID_W, h, d)
    vg = v.reshape(b, rows, GRID_W, h, d)
    ridx = krows[:, None, :, None]
    cidx = kcols[None, :, None, :]
    kb = kg[:, ridx, cidx].reshape(b, nrb, ncb, nk, h, d)
    vb = vg[:, ridx, cidx].reshape(b, nrb, ncb, nk, h, d)

    shp_q = (nrb, ncb, rb, Q_COL_BLOCK)
    shp_k = (nrb, ncb, kspan, K_COL_SPAN)
    qr = np.broadcast_to((r0[:, None] + np.arange(rb)[None, :])[:, None, :, None], shp_q).reshape(nrb, ncb, nq)
    qc = np.broadcast_to((c0[:, None] + np.arange(Q_COL_BLOCK)[None, :])[None, :, None, :], shp_q).reshape(nrb, ncb, nq)
    kr = np.broadcast_to(krows[:, None, :, None], shp_k).reshape(nrb, ncb, nk)
    kc = np.broadcast_to(kcols[None, :, None, :], shp_k).reshape(nrb, ncb, nk)
    qr, qc = qr[..., :, None], qc[..., :, None]
    kr, kc = kr[..., None, :], kc[..., None, :]
    rs = np.clip(qr - kh // 2, 0, rows - kh)
    cs = np.clip(qc - WIN_W // 2, 0, GRID_W - WIN_W)
    valid = (kr >= rs) & (kr < rs + kh) & (kc >= cs) & (kc < cs + WIN_W)
    idx_r = np.clip(kr - qr + WIN_H - 1, 0, 2 * WIN_H - 2).astype(np.int32)
    idx_c = np.clip(kc - qc + WIN_W - 1, 0, 2 * WIN_W - 2).astype(np.int32)
    bias = rpb.astype(jnp.float32)[:, idx_r, idx_c].transpose(1, 2, 0, 3, 4)

    s = jnp.einsum('bncqhd,bnckhd->bnchqk', qb, kb,
                   preferred_element_type=jnp.float32) * (1.0 / math.sqrt(d))
    s = jnp.where(jnp.asarray(valid)[None, :, :, None], s + bias[None], NEG)
    p = jax.nn.softmax(s, axis=-1)
    o = jnp.einsum('bnchqk,bnckhd->bncqhd', p.astype(v.dtype), vb)
    o = o.reshape(b, nrb, ncb, rb, Q_COL_BLOCK, h, d).transpose(0, 1, 3, 2, 4, 5, 6)
    o = o.reshape(b, rows_p, GRID_W, h * d)[:, :rows]
    return o.reshape(b, s_len, h * d)


def setup_inputs(seed: int = 0) -> dict:
    key = jax.random.key(seed)
    ks = jax.random.split(key, 20)
    f32 = jnp.float32

    def nrm(k, shape, scale):
        return jax.random.normal(k, shape, f32) * scale

    L = DEPTH
    return {
        "x": jax.random.normal(ks[0], (BATCH, SEQ, D_MODEL), f32),
        "norm_mix": 1.0 + nrm(ks[1], (L, D_MODEL), 0.02),
        "w_in": nrm(ks[2], (L, D_MODEL, IN_COLS), D_MODEL ** -0.5),
        "a_q_norm": 1.0 + nrm(ks[3], (L, HEAD_DIM), 0.02),
        "a_k_norm": 1.0 + nrm(ks[4], (L, HEAD_DIM), 0.02),
        "a_sink": nrm(ks[5], (L, A_Q_HEADS), 0.5),
        "b_q_norm": 1.0 + nrm(ks[6], (L, HEAD_DIM), 0.02),
        "b_k_norm": 1.0 + nrm(ks[7], (L, HEAD_DIM), 0.02),
        "b_rpb": nrm(ks[8], (L, B_HEADS, 2 * WIN_H - 1, 2 * WIN_W - 1), 0.1),
        "w_branch_a": nrm(ks[9], (L, A_Q_DIM, D_MODEL), A_Q_DIM ** -0.5),
        "w_branch_b": nrm(ks[10], (L, B_DIM, D_MODEL), B_DIM ** -0.5),
        "w_out": nrm(ks[11], (L, D_MODEL, D_MODEL), D_MODEL ** -0.5),
        "norm_ffn": 1.0 + nrm(ks[12], (L, D_MODEL), 0.02),
        "w_up": nrm(ks[13], (L, D_MODEL, 2 * D_FF), D_MODEL ** -0.5),
        "conv_w": nrm(ks[14], (L, CONV_W, 2 * D_FF), CONV_W ** -0.5),
        "conv_b": nrm(ks[15], (L, 2 * D_FF), 0.01),
        "w_down": nrm(ks[16], (L, D_FF, D_MODEL), D_FF ** -0.5),
    }


def reference(x, norm_mix, w_in, a_q_norm, a_k_norm, a_sink, b_q_norm, b_k_norm, b_rpb,
              w_branch_a, w_branch_b, w_out, norm_ffn, w_up, conv_w, conv_b, w_down):
    b, s_len, _ = x.shape
    pos = jnp.arange(s_len, dtype=jnp.float32)
    split_points = np.cumsum(IN_SPLITS)[:-1].tolist()
    for l in range(DEPTH):
        h = rmsnorm(x, norm_mix[l])
        proj = h @ w_in[l]
        qa, ka, va, qb, kb, vb, ga, gb = jnp.split(proj, split_points, axis=-1)

        qa = rope(rmsnorm(qa.reshape(b, s_len, A_Q_HEADS, HEAD_DIM), a_q_norm[l]), pos)
        ka = rope(rmsnorm(ka.reshape(b, s_len, A_KV_HEADS, HEAD_DIM), a_k_norm[l]), pos)
        va = va.reshape(b, s_len, A_KV_HEADS, HEAD_DIM)
        out_a = window_attention(qa, ka, va, a_sink[l])

        qb = rmsnorm(qb.reshape(b, s_len, B_HEADS, HEAD_DIM), b_q_norm[l])
        kb = rmsnorm(kb.reshape(b, s_len, B_HEADS, HEAD_DIM), b_k_norm[l])
        vb = vb.reshape(b, s_len, B_HEADS, HEAD_DIM)
        out_b = neighbourhood_attention(qb, kb, vb, b_rpb[l])

        merged = (jax.nn.sigmoid(ga) * (out_a @ w_branch_a[l])
                  + jax.nn.sigmoid(gb) * (out_b @ w_branch_b[l]))
        x = x + merged @ w_out[l]

        h2 = rmsnorm(x, norm_ffn[l])
        u = h2 @ w_up[l]
        up = jnp.pad(u, ((0, 0), (1, 1), (0, 0)))
        cw = conv_w[l]
        u = up[:, :-2] * cw[0] + up[:, 1:-1] * cw[1] + up[:, 2:] * cw[2] + conv_b[l]
        gate, val = jnp.split(u, 2, axis=-1)
        x = x + (jax.nn.silu(gate) * val) @ w_down[l]
    return x
```

```python
import contextlib
import numpy as np
import concourse.bass as bass
import concourse.mybir as mybir
from concourse.bass_utils import run_bass_kernel_spmd

F32 = mybir.dt.float32
BF16 = mybir.dt.bfloat16
ALU = mybir.AluOpType
AF = mybir.ActivationFunctionType

D = 2048
S_LEN = 4096
T = 1024
NCH = 13
NKV = NCH * 128
HL = 384
NQ = T + 2
DFF = 5632
NFF = DFF // 128
INC = 8704
EPS = 1e-6
NEG = -30000.0
SCALE = 1.0 / float(np.sqrt(128.0))
NPASS = 2
TOK_CORE = NPASS * T

ENGS = ("pe", "act", "dve", "pool", "sp")
CAP = 30000


class Region:
    __slots__ = ("name", "writers", "readers")

    def __init__(self, name):
        self.name = name
        self.writers = {}
        self.readers = {}


class DSem:
    __slots__ = ("name", "count", "sem")

    def __init__(self, name):
        self.name = name
        self.count = 0
        self.sem = None


class Op:
    __slots__ = ("eng", "fn", "deps", "dsem", "dval", "sigidx", "key", "signal")

    def __init__(self, eng, fn, dsem):
        self.eng = eng
        self.fn = fn
        self.deps = set()
        self.dsem = dsem
        self.dval = 0
        self.sigidx = 0
        self.signal = False
        self.key = eng if dsem is None else ("dma", id(self))


class Sched:
    def __init__(self, nc):
        self.nc = nc
        self.ops = {e: [] for e in ENGS}
        self.pending = {e: set() for e in ENGS}
        self.dsems = []
        self.dma_live = []
        self.last = {}
        self.final = {e: set() for e in ENGS}

    def dsem(self, name):
        d = DSem(name)
        self.dsems.append(d)
        return d

    def add(self, eng, fn, reads=(), writes=(), dsem=None, join=False):
        op = Op(eng, fn, dsem)
        deps = set(self.pending[eng])
        self.pending[eng] = set()
        is_dma = dsem is not None
        for r in reads:
            for k, w in r.writers.items():
                if eng == "pe" and k == "pe":
                    continue
                deps.add(w)
        if not join:
            for r in writes:
                for k, w in r.writers.items():
                    if eng == "pe" and k == "pe":
                        continue
                    deps.add(w)
                for k, w in r.readers.items():
                    if eng == "pe" and k == "pe":
                        continue
                    deps.add(w)
        op.deps = deps
        for r in writes:
            if join:
                r.writers[op.key] = op
            else:
                r.writers = {op.key: op}
                r.readers = {}
        for r in reads:
            if r in writes:
                continue
            r.readers[op.key] = op
        if is_dma:
            dsem.count += 16
            op.dval = dsem.count
            self.dma_live.append(op)
        else:
            self.last[eng] = op
        self.ops[eng].append(op)
        return op

    def barrier(self, engines=("pe", "act", "dve", "sp")):
        deps = set()
        for e in engines:
            if e in self.last:
                deps.add(self.last[e])
        keep = []
        for op in self.dma_live:
            if op.eng in engines:
                deps.add(op)
            else:
                keep.append(op)
        self.dma_live = keep
        for e in engines:
            self.pending[e] |= {d for d in deps if d.key != e}

    def final_wait(self, eng, ops):
        self.final[eng] |= set(ops)

    def emit(self):
        nc = self.nc
        needed = set()
        for e in ENGS:
            for op in self.ops[e]:
                needed |= op.deps
            needed |= self.final[e]
        counts = {}
        for e in ENGS:
            c = 0
            for op in self.ops[e]:
                if op.dsem is None and op in needed:
                    c += 1
                    op.sigidx = c
                    op.signal = True
            counts[e] = c
        with contextlib.ExitStack() as st:
            esems = {}
            for e in ENGS:
                n = (counts[e] + CAP - 1) // CAP
                esems[e] = [st.enter_context(nc.semaphore(f"c_{e}_{i}")) for i in range(n)]
            for d in self.dsems:
                if d.count > 0:
                    d.sem = st.enter_context(nc.semaphore(f"d_{d.name}"))

            def resolve(op):
                if op.dsem is not None:
                    return op.dsem.sem, op.dval
                i = (op.sigidx - 1) // CAP
                return esems[op.eng][i], op.sigidx - i * CAP

            block = st.enter_context(nc.Block())
            sched = self

            def make_body(e):
                def body(eng):
                    waited = {}

                    def do_wait(d):
                        sem, val = resolve(d)
                        k = sem.num
                        if waited.get(k, 0) < val:
                            eng.wait_ge(sem, val)
                            waited[k] = val

                    def wait_all(deps):
                        best = {}
                        for d in deps:
                            sem, val = resolve(d)
                            if val > best.get(sem.num, (None, 0))[1]:
                                best[sem.num] = (sem, val)
                        for k in sorted(best):
                            sem, val = best[k]
                            if waited.get(k, 0) < val:
                                eng.wait_ge(sem, val)
                                waited[k] = val

                    for op in sched.ops[e]:
                        wait_all(op.deps)
                        ins = op.fn(eng)
                        if op.dsem is not None:
                            ins.then_inc(op.dsem.sem, 16)
                        elif op.signal:
                            sem, _ = resolve(op)
                            ins.then_inc(sem, 1)
                    wait_all(sched.final[e])
                return body

            hw = {"pe": block.tensor, "act": block.scalar, "dve": block.vector,
                  "pool": block.gpsimd, "sp": block.sync}
            for e in ENGS:
                if self.ops[e] or self.final[e]:
                    hw[e](make_body(e))
        return counts


def i_mm(out, lhsT, rhs, start, stop):
    return lambda e: e.matmul(out, lhsT=lhsT, rhs=rhs, start=start, stop=stop)


def i_tr(out, in_, ident):
    return lambda e: e.transpose(out=out, in_=in_, identity=ident)


def i_act(out, in_, func, scale=None, accum_out=None, bias=None):
    kw = {}
    if bias is not None:
        kw["bias"] = bias
    if scale is not None:
        kw["scale"] = scale
    if accum_out is not None:
        kw["accum_out"] = accum_out
    return lambda e: e.activation(out=out, in_=in_, func=func, **kw)


def i_ts(out, in0, s1, s2, op0, op1=None):
    if op1 is None:
        return lambda e: e.tensor_scalar(out=out, in0=in0, scalar1=s1, scalar2=None, op0=op0)
    return lambda e: e.tensor_scalar(out=out, in0=in0, scalar1=s1, scalar2=s2, op0=op0, op1=op1)


def i_stt(out, in0, scalar, in1, op0, op1):
    return lambda e: e.scalar_tensor_tensor(out=out, in0=in0, scalar=scalar, in1=in1, op0=op0, op1=op1)


def i_tt(out, in0, in1, op):
    return lambda e: e.tensor_tensor(out=out, in0=in0, in1=in1, op=op)


def i_vcopy(out, in_):
    return lambda e: e.tensor_copy(out=out, in_=in_)


def i_recip(out, in_):
    return lambda e: e.reciprocal(out=out, in_=in_)


def i_dma(out, in_):
    return lambda e: e.dma_start(out=out, in_=in_)


def build_program(npass=NPASS, stop_after=None, debug=()):
    nc = bass.Bass("TRN2", target_bir_lowering=False)
    S = Sched(nc)

    def din(name, shape, dt=F32):
        return nc.dram_tensor(name, list(shape), dt, kind="ExternalInput").ap()

    xe = din("xe", [NPASS, NKV, D])
    w_in = din("w_in", [D, INC])
    w_ba = din("w_ba", [1024, D])
    w_bb = din("w_bb", [1024, D])
    w_out = din("w_out", [D, D])
    w_up = din("w_up", [D, 2 * DFF])
    w_down = din("w_down", [DFF, D])
    gmix_d = din("gmix_b", [128, D])
    gffn_d = din("gffn_b", [128, D])
    gains_d = din("gains", [128, 4])
    sink_d = din("sink_b", [128, 8])
    cw_d = din("cw_fm", [128, 88 * 3])
    cb_d = din("cb_fm", [128, 88])
    ident_d = din("ident", [128, 128])
    rm_d = din("rm", [128, 128])
    cos_d = din("cos_t", [NPASS, 128, NKV])
    sin_d = din("sin_t", [NPASS, 128, NKV])
    maskA_d = din("maskA", [NPASS, 128, 9 * 128])
    tabB_d = din("tabB", [NPASS, 8, 128, 27 * 128])
    tabBx_d = din("tabBx", [NPASS, 128, 72])
    uflag_d = din("uflag", [128, 2 * NPASS])
    out_d = nc.dram_tensor("out", [TOK_CORE, D], F32, kind="ExternalOutput").ap()
    x1scr = nc.dram_tensor("x1scr", [TOK_CORE, D], F32).ap()
    dbg_out = {}

    st = contextlib.ExitStack()
    with st:
        def sb(name, shape, dt):
            return st.enter_context(nc.sbuf_tensor("sb_" + name, list(shape), dt))

        ident = sb("ident", [128, 128], BF16)
        ident_f = sb("ident_f", [128, 128], F32)
        ones_f = sb("ones_f", [128, 128], F32)
        ones_b = sb("ones_b", [128, 128], BF16)
        ones_ms = sb("ones_ms", [128, 128], BF16)
        rm_f = sb("rm_f", [128, 128], F32)
        rm_b = sb("rm_b", [128, 128], BF16)
        gains = sb("gains", [128, 4], F32)
        esink = sb("esink", [128, 8], F32)
        cw = sb("cw", [128, 88 * 3], F32)
        cb = sb("cb", [128, 88], F32)
        tabBx = sb("tabBx", [128, 72], F32)
        ssb = sb("ssb", [128, 4], F32)
        rstdb = sb("rstdb", [128, 4], F32)
        epsc = sb("epsc", [128, 1], F32)
        uflag = sb("uflag", [128, 2 * NPASS], F32)
        HT = sb("HT", [128, 16 * NKV], BF16)
        WR = [sb(f"WR{i}", [128, 8192], BF16) for i in range(3)]
        NBIG = 51000
        BIG = sb("BIG", [128, NBIG], BF16)
        ps = st.enter_context(nc.psum_tensor("ps", [128, 8, 512], F32))
        psb = ps.bitcast(BF16)

        OA0 = NBIG - 2 * 8208
        OB0 = NBIG - 8208

        def carve(off, n, dt=BF16):
            v = BIG[:, off:off + n]
            return v.bitcast(F32) if dt == F32 else v

        hT3 = HT[:, :].rearrange("p (k t) -> p k t", k=16)
        out_aT = BIG[:, OA0:OA0 + 8208].rearrange("p (h t) -> p h t", h=8)
        out_bT = BIG[:, OB0:OB0 + 8208].rearrange("p (h t) -> p h t", h=8)

        R = {}

        def reg(name):
            if name not in R:
                R[name] = Region(name)
            return R[name]

        R_bank = [reg(f"bank{b}") for b in range(8)]
        R_w = [reg(f"w{i}") for i in range(3)]
        D_w = [S.dsem(f"w{i}") for i in range(3)]
        R_hT = [reg(f"hT{c}") for c in range(NCH)]
        R_const = reg("const")
        D_const = S.dsem("const")
        bank_ctr = [0]

        held = set()

        def next_bank(hold=False):
            while True:
                b = bank_ctr[0] % 8
                bank_ctr[0] += 1
                if b not in held:
                    break
            if hold:
                held.add(b)
            return b

        wctr = [0]

        def wload(parts):
            s = wctr[0] % 3
            wctr[0] += 1
            for i, (dst, src) in enumerate(parts):
                S.add("pool", i_dma(dst(WR[s]), src), writes=[R_w[s]], dsem=D_w[s], join=(i > 0))
            return s

        def wv(s, nk, ncol):
            return WR[s][:, 0:nk * ncol].rearrange("p (k m) -> p k m", k=nk)

        def cload(dst, src):
            S.add("sp", i_dma(dst, src), writes=[R_const], dsem=D_const, join=True)

        cload(ident_f[:], ident_d)
        cload(rm_f[:], rm_d)
        cload(gains[:], gains_d)
        cload(esink[:], sink_d)
        cload(cw[:], cw_d)
        cload(cb[:], cb_d)
        cload(uflag[:], uflag_d)
        S.barrier()
        S.add("dve", i_vcopy(ident[:], ident_f[:]), reads=[R_const], writes=[R_const])
        S.add("dve", i_vcopy(rm_b[:], rm_f[:]), writes=[R_const])
        S.add("dve", lambda e: e.memset(ones_f[:], 1.0 / 128.0), writes=[R_const])
        S.add("dve", lambda e: e.memset(ones_b[:], 1.0), writes=[R_const])
        S.add("dve", lambda e: e.memset(ones_ms[:], 1.0 / 128.0), writes=[R_const])
        S.add("dve", lambda e: e.memset(epsc[:], EPS), writes=[R_const])
        S.add("act", i_act(esink[:], esink[:], AF.Exp), reads=[R_const], writes=[R_const])
        S.barrier()

        qcols_h = [(HL, HL + 512), (HL + 512, HL + 1024)]

        def hq(kc, n):
            if n < 2:
                return hT3[:, kc, qcols_h[n][0]:qcols_h[n][1]]
            return hT3[:, kc, HL - 1:HL + T + 1:T + 1]

        def hq_regs(n):
            if n == 0:
                return R_hT[3:7]
            if n == 1:
                return R_hT[7:11]
            return [R_hT[2], R_hT[11]]

        QN = [512, 512, 2]
        QO = [(0, 512), (512, 1024), (1024, 1026)]
        KVN = [(0, 512), (512, 1024), (1024, 1536), (1536, 1664)]

        def kv_regs(n):
            lo, hi = KVN[n]
            return R_hT[lo // 128:(hi + 127) // 128]

        def tcol(ap, n):
            if n < 2:
                return ap[:, qcols_h[n][0]:qcols_h[n][1]]
            return ap[:, HL - 1:HL + T + 1:T + 1]

        final_ops = []

        def norm_stage1(slot, xt, xn, junk, gbt, rows, pfx="xt", gbreg="gb"):
            r_xt, r_xn = reg(f"{pfx}{slot}"), reg(f"{pfx}n{slot}")
            r_ss, r_rs = reg(f"ss{slot}"), reg(f"rs{slot}")
            jk = junk if junk is not None else xn[slot]
            jr = reg("junk") if junk is not None else r_xn
            S.add("act", i_act(jk[0:rows, :], xt[slot][0:rows, :], AF.Square,
                               accum_out=ssb[0:rows, slot:slot + 1]),
                  reads=[r_xt], writes=[jr, r_ss])
            S.add("act", i_act(rstdb[0:rows, slot:slot + 1], ssb[0:rows, slot:slot + 1], AF.Ln,
                               scale=1.0 / D, bias=epsc[0:rows, 0:1]), reads=[r_ss, R_const], writes=[r_rs])
            S.add("act", i_act(rstdb[0:rows, slot:slot + 1], rstdb[0:rows, slot:slot + 1], AF.Exp, scale=-0.5),
                  reads=[r_rs], writes=[r_rs])
            S.add("dve", i_stt(xn[slot][0:rows, :], xt[slot][0:rows, :], rstdb[0:rows, slot:slot + 1],
                               gbt[0:rows, :], ALU.mult, ALU.mult),
                  reads=[r_xt, r_rs, reg(gbreg)], writes=[r_xn])

        def norm_stage2(slot, xn, rows, dst_fn, dst_regs, ei, pfx="xt"):
            r_xn = reg(f"{pfx}n{slot}")
            for half in range(2):
                b = next_bank()
                for k in range(8):
                    kk = half * 8 + k
                    S.add("pe", i_tr(psb[:, b, k * 128:k * 128 + rows], xn[slot][0:rows, kk * 128:(kk + 1) * 128],
                                     ident[0:rows, 0:rows]),
                          reads=[r_xn, R_const], writes=[R_bank[b]])
                src = psb[:, b, 0:1024].rearrange("p (k t) -> p k t", k=8)[:, :, 0:rows]
                eng = "act" if (ei + half) % 2 == 0 else "dve"
                if eng == "act":
                    S.add("act", i_act(dst_fn(half), src, AF.Copy), reads=[R_bank[b]], writes=dst_regs)
                else:
                    S.add("dve", i_vcopy(dst_fn(half), src), reads=[R_bank[b]], writes=dst_regs)


        def wsrc(w, c0, n):
            return w[:, c0:c0 + n].rearrange("(k p) m -> p k m", p=128)

        def wload2(nk, ncol, parts):
            s_ = wctr[0] % 3
            wctr[0] += 1
            w3 = wv(s_, nk, ncol)
            for i, (k0, k1, c0, src) in enumerate(parts):
                n = src.shape[-1]
                S.add("pool", i_dma(w3[:, k0:k1, c0:c0 + n], src), writes=[R_w[s_]], dsem=D_w[s_], join=(i > 0))
            return s_

        def gemm_fm(ws, nk, wcols, coff, k0, rhs_fn, rhs_regs_fn, nlist, consume):
            w3 = wv(ws, nk, wcols)
            nkk = nlist[0][2] if len(nlist[0]) > 2 else None
            for (n, nn) in nlist:
                b = next_bank()
                cnt = 16
                for kc in range(cnt):
                    S.add("pe", i_mm(ps[:, b, 0:nn], w3[:, k0 + kc, coff:coff + 128], rhs_fn(kc, n), kc == 0, kc == cnt - 1),
                          reads=[R_w[ws]] + rhs_regs_fn(n), writes=[R_bank[b]])
                consume(n, b)

        def mk_tmp(offs, pfx):
            d = {}
            for nm, (o, n, dt) in offs.items():
                d[nm] = carve(o, n, dt)
                d["r_" + nm] = reg(pfx + nm)
            return d

        class Pipe:
            def __init__(self):
                self.live = []

            def push(self, gen):
                try:
                    next(gen)
                    self.live.append(gen)
                except StopIteration:
                    pass

            def tick(self):
                nl = []
                for g_ in self.live:
                    try:
                        next(g_)
                        nl.append(g_)
                    except StopIteration:
                        pass
                self.live = nl

            def flush(self):
                while self.live:
                    self.tick()

        pipe = Pipe()
        pending_units = []
        unit_ctr = [0]

        round_ctr = [0]
        deferred = []

        def defer_units(delay, items):
            deferred.append((round_ctr[0] + delay, items))

        def round_end(new_gen=None, nunits=0, tmps=None):
            pipe.tick()
            round_ctr[0] += 1
            while deferred and deferred[0][0] <= round_ctr[0]:
                pending_units.extend(deferred.pop(0)[1])
            if new_gen is not None:
                pipe.push(new_gen)
            k = 0
            while k < nunits and pending_units:
                u = pending_units.pop(0)
                if isinstance(u, tuple):
                    u[1]()
                    continue
                t = tmps[unit_ctr[0] % len(tmps)]
                unit_ctr[0] += 1
                pipe.push(unit_gen(u, t))
                k += 1

        def drain_units(nunits, tmps):
            while deferred or pending_units:
                round_end(None, nunits=nunits, tmps=tmps)
            pipe.flush()

        def gemm_tile(ws, nk, wcols, coff, rhs_fn, rhs_regs, nn):
            w3 = wv(ws, nk, wcols)
            b = next_bank(hold=True)
            for kc in range(16):
                S.add("pe", i_mm(ps[:, b, 0:nn], w3[:, kc, coff:coff + 128], rhs_fn(kc), kc == 0, kc == 15),
                      reads=[R_w[ws]] + rhs_regs, writes=[R_bank[b]])
            return b

        def qknorm_gen(b, n, gain_ap, out_ap, out_regs, tmp, rope=None):
            psv = ps[:, b, 0:n]
            sqb = tmp["sq"].bitcast(BF16)
            S.add("act", i_act(sqb[:, 0:n], psv, AF.Square), reads=[R_bank[b]], writes=[tmp["r_sq"]])
            yield
            b2 = next_bank()
            S.add("pe", i_mm(ps[:, b2, 0:n], ones_ms[:, :], sqb[:, 0:n], True, True),
                  reads=[tmp["r_sq"], R_const], writes=[R_bank[b2]])
            S.add("act", i_act(tmp["rstd"][:, 0:n], ps[:, b2, 0:n], AF.Ln, bias=epsc[:, 0:1]),
                  reads=[R_bank[b2], R_const], writes=[tmp["r_rstd"]])
            S.add("act", i_act(tmp["rstd"][:, 0:n], tmp["rstd"][:, 0:n], AF.Exp, scale=-0.5),
                  reads=[tmp["r_rstd"]], writes=[tmp["r_rstd"]])
            if rope is None:
                S.add("dve", i_stt(out_ap, psv, gain_ap, tmp["rstd"][:, 0:n], ALU.mult, ALU.mult),
                      reads=[R_bank[b], tmp["r_rstd"], R_const], writes=out_regs)
                held.discard(b)
                return
            cos_ap, sin_ap = rope
            y32, t1, t2 = tmp["y32"], tmp["t1"], tmp["t2"]
            yb = tmp["sq"].bitcast(BF16)[:, 512:1024]
            r_yb = reg(tmp["r_sq"].name + "_yb")
            S.add("dve", i_stt(y32[:, 0:n], psv, gain_ap, tmp["rstd"][:, 0:n], ALU.mult, ALU.mult),
                  reads=[R_bank[b], tmp["r_rstd"], R_const], writes=[tmp["r_y32"]])
            held.discard(b)
            S.add("act", i_act(yb[:, 0:n], y32[:, 0:n], AF.Copy), reads=[tmp["r_y32"]], writes=[r_yb])
            S.add("pool", i_tt(t1[:, 0:n], y32[:, 0:n], cos_ap, ALU.mult),
                  reads=[tmp["r_y32"], reg("rope")], writes=[tmp["r_t1"]])
            yield
            b3 = next_bank()
            S.add("pe", i_mm(ps[:, b3, 0:n], rm_b[:, :], yb[:, 0:n], True, True),
                  reads=[r_yb, R_const], writes=[R_bank[b3]])
            S.add("dve", i_tt(t2[:, 0:n], ps[:, b3, 0:n], sin_ap, ALU.mult),
                  reads=[R_bank[b3], reg("rope")], writes=[tmp["r_t2"]])
            S.add("pool", i_tt(out_ap, t1[:, 0:n], t2[:, 0:n], ALU.add),
                  reads=[tmp["r_t1"], tmp["r_t2"]], writes=out_regs)

        def vtr_gen(vT, r_vT, Vdst, r_V):
            yield
            for gi, (c0, c1) in enumerate(((0, 8), (8, 13))):
                b = next_bank()
                for c in range(c0, c1):
                    S.add("pe", i_tr(psb[:, b, (c - c0) * 128:(c - c0 + 1) * 128], vT[:, c * 128:(c + 1) * 128], ident[:, :]),
                          reads=[r_vT, R_const], writes=[R_bank[b]])
                src = psb[:, b, 0:(c1 - c0) * 128].rearrange("p (c d) -> p c d", c=c1 - c0)
                if gi == 0:
                    S.add("dve", i_vcopy(Vdst[:, c0:c1, :], src), reads=[R_bank[b]], writes=[r_V])
                    yield
                else:
                    S.add("act", i_act(Vdst[:, c0:c1, :], src, AF.Copy), reads=[R_bank[b]], writes=[r_V])

        def unit_gen(u, t):
            attn_s1(u, t)
            yield
            attn_s2(u, t)

        def attn_s1(u, t):
            nch, nq = len(u["kch"]), u["nq"]
            P = t["P"].rearrange("p (j q) -> p j q", q=128)
            if u.get("mask_pe") is not None:
                b = next_bank()
                for j in range(nch):
                    S.add("pe", i_mm(ps[:, b, j * nq:(j + 1) * nq], u["kch"][j], u["q"], True, False),
                          reads=u["regs_qk"], writes=[R_bank[b]])
                    S.add("pe", i_mm(ps[:, b, j * nq:(j + 1) * nq], ident[:, :], u["mask_pe"][j], False, True),
                          reads=u["regs_tab"] + [R_const], writes=[R_bank[b]])
                src = ps[:, b, 0:nch * nq].rearrange("p (j q) -> p j q", j=nch)
                S.add("act", i_act(P[:, 0:nch, 0:nq], src, AF.Exp, scale=SCALE), reads=[R_bank[b]], writes=[t["r_P"]])
                return
            E = t["E"].rearrange("p (j q) -> p j q", q=128)
            for g0 in range(0, nch, 4):
                g1 = min(nch, g0 + 4)
                b = next_bank()
                for j in range(g0, g1):
                    S.add("pe", i_mm(ps[:, b, (j - g0) * nq:(j - g0 + 1) * nq], u["kch"][j], u["q"], True, True),
                          reads=u["regs_qk"], writes=[R_bank[b]])
                src = ps[:, b, 0:(g1 - g0) * nq].rearrange("p (j q) -> p j q", j=g1 - g0)
                S.add("dve", i_stt(E[:, g0:g1, 0:nq], src, SCALE, u["tab"][:, g0:g1, :], ALU.mult, ALU.add),
                      reads=[R_bank[b]] + u["regs_tab"], writes=[t["r_E"]])
            S.add("act", i_act(P[:, 0:nch, 0:nq], E[:, 0:nch, 0:nq], AF.Exp), reads=[t["r_E"]], writes=[t["r_P"]])

        def attn_s2(u, t):
            nch, nq = len(u["kch"]), u["nq"]
            P = t["P"].rearrange("p (j q) -> p j q", q=128)
            rz = t["rz"]
            b = next_bank()
            for j in range(nch):
                S.add("pe", i_mm(ps[:, b, 0:nq], u["vch"][j], P[:, j, 0:nq], j == 0, j == nch - 1),
                      reads=[t["r_P"]] + u["regs_v"], writes=[R_bank[b]])
            for j in range(nch):
                S.add("pe", i_mm(ps[:, b, 128:128 + nq], ones_b[:, :], P[:, j, 0:nq], j == 0, j == nch - 1),
                      reads=[t["r_P"], R_const], writes=[R_bank[b]])
            if u["sink"] is not None:
                S.add("dve", i_ts(rz[:, 0:nq], ps[:, b, 128:128 + nq], u["sink"], None, ALU.add),
                      reads=[R_bank[b], R_const], writes=[t["r_rz"]])
                S.add("dve", i_recip(rz[:, 0:nq], rz[:, 0:nq]), reads=[t["r_rz"]], writes=[t["r_rz"]])
            else:
                S.add("dve", i_recip(rz[:, 0:nq], ps[:, b, 128:128 + nq]), reads=[R_bank[b]], writes=[t["r_rz"]])
            S.add("dve", i_tt(u["out"], ps[:, b, 0:nq], rz[:, 0:nq], ALU.mult),
                  reads=[R_bank[b], t["r_rz"]], writes=u["regs_out"])

        def run_units(units, tmps):
            prev = None
            for i, u in enumerate(units):
                attn_s1(u, tmps[i % 2])
                if prev is not None:
                    attn_s2(prev[0], prev[1])
                prev = (u, tmps[i % 2])
            if prev is not None:
                attn_s2(prev[0], prev[1])

        def dump(name, ap, regs, shape, dt):
            if name in debug:
                dbg_out[name] = nc.dram_tensor("dbg_" + name, list(shape), dt, kind="ExternalOutput").ap()
                final_ops.append(S.add("sp", i_dma(dbg_out[name], ap), reads=regs, dsem=S.dsem("dbg_" + name)))

        R_oa, R_ob = reg("oa"), reg("ob")
        R_mg = [reg(f"mg{n}") for n in range(3)]
        R_h2T = [reg(f"h2T{t}") for t in range(9)]
        R_act = reg("act")
        R_x1scr = [[reg(f"x1scr{p}_{t}") for t in range(8)] for p in range(NPASS)]
        D_cos, D_sin, D_mask = S.dsem("cos"), S.dsem("sin"), S.dsem("maskA")
        D_tabB = [S.dsem(f"tabB{k}") for k in range(5)]
        R_tabB = [reg(f"tabB{k}") for k in range(5)]
        D_tabBx = S.dsem("tabBx")
        D_xt2 = [S.dsem(f"xt2_{i}") for i in range(2)]
        D_x1st = [S.dsem(f"x1st_{i}") for i in range(2)]
        D_gb2 = S.dsem("gb2")
        D_oa = S.dsem("oa_w")
        D_x1s = [S.dsem(f"x1s{i}") for i in range(8)]
        D_yst = [S.dsem(f"yst{i}") for i in range(4)]
        D_xtA = [S.dsem(f"xtA{i}") for i in range(2)]
        D_gbA = S.dsem("gbA")

        for p in range(npass):
            xt = [carve(0, 4096, F32), carve(4096, 4096, F32)]
            xn = [carve(8192, 2048), carve(10240, 2048)]
            junk = carve(12288, 2048)
            gbt = carve(14336, 4096, F32)
            D_xt = D_xtA
            D_gb = D_gbA
            S.barrier()
            S.add("sp", i_dma(gbt[:, :], gmix_d), writes=[reg("gb")], dsem=D_gb)
            prev = None
            for c in range(NCH + 1):
                if c < NCH:
                    s = c % 2
                    S.add("sp", i_dma(xt[s][:, :], xe[p, c * 128:(c + 1) * 128, :]),
                          writes=[reg(f"xt{s}")], dsem=D_xt[s])
                    norm_stage1(s, xt, xn, junk, gbt, 128)
                if prev is not None:
                    cc, ss_ = prev
                    norm_stage2(ss_, xn, 128,
                                lambda half, cc=cc: hT3[:, half * 8:(half + 1) * 8, cc * 128:(cc + 1) * 128],
                                [R_hT[cc]], cc)
                prev = (c, c % 2) if c < NCH else None
            if "hT" in debug and p == 0:
                dbg_out["hT"] = nc.dram_tensor("dbg_hT", [128, 16 * NKV], BF16, kind="ExternalOutput").ap()
                final_ops.append(S.add("sp", i_dma(dbg_out["hT"], HT[:, :]), reads=R_hT, dsem=S.dsem("dbg_hT")))
            if stop_after == "A":
                break

            S.barrier()
            cosb = carve(0, 3328, F32)
            sinb = carve(3328, 3328, F32)
            maskA = carve(6656, 2304, F32).rearrange("p (a j q) -> p a j q", a=3, j=3)
            qA = carve(8960, 4104).rearrange("p (h t) -> p h t", h=4)
            kTa = carve(13064, 1664)
            vTa = carve(14728, 1664)
            Va = carve(16392, 1664).rearrange("p (c d) -> p c d", c=13)
            tmpA = [mk_tmp(dict(sq=(18056, 1024, F32), rstd=(19080, 1024, F32), y32=(20104, 1024, F32),
                                t1=(21128, 1024, F32), t2=(22152, 1024, F32)), "tA0"),
                    mk_tmp(dict(sq=(23176, 1024, F32), rstd=(24200, 1024, F32), y32=(25224, 1024, F32),
                                t1=(26248, 1024, F32), t2=(27272, 1024, F32)), "tA1")]
            atA = [mk_tmp(dict(P=(28296 + i * 640, 384, BF16), rz=(28296 + i * 640 + 384, 256, F32)), f"aA{i}")
                   for i in range(4)]
            maskbf = carve(30856, 1152).rearrange("p (a j q) -> p a j q", a=3, j=3)
            r_qA = [reg(f"qA{i}") for i in range(4)]
            r_kTa, r_vTa, r_Va = reg("kTa"), reg("vTa"), reg("Va")
            S.add("sp", i_dma(cosb[:, :], cos_d[p]), writes=[reg("rope")], dsem=D_cos)
            S.add("sp", i_dma(sinb[:, :], sin_d[p]), writes=[reg("rope")], dsem=D_sin, join=True)
            S.add("sp", i_dma(carve(6656, 2304, F32), maskA_d[p]), writes=[reg("maskA32")], dsem=D_mask)
            S.add("dve", i_vcopy(carve(30856, 1152), carve(6656, 2304, F32)), reads=[reg("maskA32")], writes=[reg("maskA")])
            tq = [0]

            def nxt_tmpA():
                tq[0] += 1
                return tmpA[tq[0] % 2]

            kv_list = [(n, KVN[n][1] - KVN[n][0]) for n in range(4)]
            q_list = [(0, 512), (1, 512), (2, 2)]
            def load_kv(g_):
                return wload2(16, 256, [(0, 16, 0, wsrc(w_in, 1024 + g_ * 128, 128)),
                                        (0, 16, 128, wsrc(w_in, 1280 + g_ * 128, 128))])

            ws_kv_next = load_kv(0)
            for g in range(2):
                ws = ws_kv_next
                ws2 = wload2(16, 512, [(0, 16, 0, wsrc(w_in, g * 512, 512))])
                for (n, nn) in kv_list:
                    lo, hi = KVN[n]
                    b = gemm_tile(ws, 16, 256, 0, lambda kc, lo=lo, hi=hi: hT3[:, kc, lo:hi], kv_regs(n), nn)
                    round_end(qknorm_gen(b, nn, gains[:, 1:2], kTa[:, lo:hi], [r_kTa], nxt_tmpA(),
                                         rope=(cosb[:, lo:hi], sinb[:, lo:hi])))
                for (n, nn) in kv_list:
                    lo, hi = KVN[n]
                    b = gemm_tile(ws, 16, 256, 128, lambda kc, lo=lo, hi=hi: hT3[:, kc, lo:hi], kv_regs(n), nn)
                    S.add("act", i_act(vTa[:, lo:hi], ps[:, b, 0:nn], AF.Copy), reads=[R_bank[b]], writes=[r_vTa])
                    held.discard(b)
                    round_end(None)
                pipe.push(vtr_gen(vTa, r_vTa, Va, r_Va))
                if g == 0:
                    ws_kv_next = load_kv(1)
                for hh in range(4):
                    for (n, nn) in q_list:
                        b = gemm_tile(ws2, 16, 512, hh * 128, lambda kc, n=n: hq(kc, n), hq_regs(n), nn)
                        round_end(qknorm_gen(b, nn, gains[:, 0:1], qA[:, hh, QO[n][0]:QO[n][1]], [r_qA[hh]], nxt_tmpA(),
                                             rope=(tcol(cosb, n), tcol(sinb, n))), nunits=4, tmps=atA)
                    hq_ = g * 4 + hh
                    batch = []
                    for i in range(8):
                        c = 3 + i
                        kind = 0 if i == 0 else (2 if i == 7 else 1)
                        batch.append(dict(q=qA[:, hh, i * 128:(i + 1) * 128], nq=128,
                                                  kch=[kTa[:, cc * 128:(cc + 1) * 128] for cc in (c - 1, c, c + 1)],
                                                  vch=[Va[:, cc, :] for cc in (c - 1, c, c + 1)],
                                                  tab=None, mask_pe=[maskbf[:, kind, j, :] for j in range(3)],
                                                  sink=esink[:, hq_:hq_ + 1],
                                                  out=out_aT[:, hq_, i * 128:(i + 1) * 128],
                                                  regs_qk=[r_qA[hh], r_kTa], regs_v=[r_Va], regs_tab=[reg("maskA")],
                                                  regs_out=[R_oa]))
                    for (col, chunks, qi) in ((1024, (1, 2, 3), 127), (1025, (10, 11, 12), 0)):
                        batch.append(dict(q=qA[:, hh, col:col + 1], nq=1,
                                                  kch=[kTa[:, cc * 128:(cc + 1) * 128] for cc in chunks],
                                                  vch=[Va[:, cc, :] for cc in chunks],
                                                  tab=None, mask_pe=[maskbf[:, 1, j, qi:qi + 1] for j in range(3)],
                                                  sink=esink[:, hq_:hq_ + 1],
                                                  out=out_aT[:, hq_, col:col + 1],
                                                  regs_qk=[r_qA[hh], r_kTa], regs_v=[r_Va], regs_tab=[reg("maskA")],
                                                  regs_out=[R_oa]))
                    defer_units(3, batch)
                drain_units(2, atA)
                if g == 0 and p == 0:
                    dump("kTa", kTa[:, :], [r_kTa], [128, 1664], BF16)
                    dump("qA", carve(8960, 4104), r_qA, [128, 4104], BF16)
                    dump("Va", carve(16392, 1664), [r_Va], [128, 1664], BF16)
            if p == 0:
                dump("oa", BIG[:, OA0:OA0 + 8208], [R_oa], [128, 8208], BF16)
            if stop_after == "B":
                break

            S.barrier()
            tabB = carve(0, 6912, F32).rearrange("p (j q) -> p j q", j=27)
            qTb = [carve(6912, 1026), carve(7938, 1026)]
            kTb = [carve(8964, 1664), carve(10628, 1664)]
            vTb = carve(12292, 1664)
            Vb = [carve(13956, 1664).rearrange("p (c d) -> p c d", c=13),
                  carve(15620, 1664).rearrange("p (c d) -> p c d", c=13)]
            tmpB = [mk_tmp(dict(sq=(17284, 1024, F32), rstd=(18308, 1024, F32)), "tB0"),
                    mk_tmp(dict(sq=(19332, 1024, F32), rstd=(20356, 1024, F32)), "tB1")]
            atB = [mk_tmp(dict(E=(21380 + i * 2560, 1536, F32), P=(21380 + i * 2560 + 1536, 768, BF16),
                               rz=(21380 + i * 2560 + 2304, 256, F32)), f"aB{i}") for i in range(4)]
            S.add("sp", i_dma(tabBx[:, :], tabBx_d[p]), writes=[reg("tabBx")], dsem=D_tabBx)
            tqb = [0]

            def nxt_tmpB():
                tqb[0] += 1
                return tmpB[tqb[0] % 2]

            for h in range(8):
                s_ = h % 2
                r_q, r_k, r_vT, r_V = reg(f"qTb{s_}"), reg(f"kTb{s_}"), reg("vTb"), reg(f"Vb{s_}")
                ws = wload2(16, 384, [(0, 16, 0, wsrc(w_in, 1536 + h * 128, 128)),
                                      (0, 16, 128, wsrc(w_in, 2560 + h * 128, 128)),
                                      (0, 16, 256, wsrc(w_in, 3584 + h * 128, 128))])
                for (n, nn) in kv_list:
                    lo, hi = KVN[n]
                    b = gemm_tile(ws, 16, 384, 128, lambda kc, lo=lo, hi=hi: hT3[:, kc, lo:hi], kv_regs(n), nn)
                    round_end(qknorm_gen(b, nn, gains[:, 3:4], kTb[s_][:, lo:hi], [r_k], nxt_tmpB()), nunits=2, tmps=atB)
                for (n, nn) in kv_list:
                    lo, hi = KVN[n]
                    b = gemm_tile(ws, 16, 384, 256, lambda kc, lo=lo, hi=hi: hT3[:, kc, lo:hi], kv_regs(n), nn)
                    S.add("act", i_act(vTb[:, lo:hi], ps[:, b, 0:nn], AF.Copy), reads=[R_bank[b]], writes=[r_vT])
                    held.discard(b)
                    round_end(None, nunits=2, tmps=atB)
                pipe.push(vtr_gen(vTb, r_vT, Vb[s_], r_V))
                for (n, nn) in q_list:
                    b = gemm_tile(ws, 16, 384, 0, lambda kc, n=n: hq(kc, n), hq_regs(n), nn)
                    round_end(qknorm_gen(b, nn, gains[:, 2:3], qTb[s_][:, QO[n][0]:QO[n][1]], [r_q], nxt_tmpB()),
                              nunits=2, tmps=atB)
                assert not pending_units and not deferred, "units of the previous head must be registered by now"

                def load_tabs(h=h):
                    for kind in range(5):
                        off = B_KOFF[kind]
                        nchk = len(B_KINDS[kind][1])
                        S.add("sp", i_dma(tabB[:, off:off + nchk, :],
                                          tabB_d[p, h, :, off * 128:(off + nchk) * 128].rearrange("p (j q) -> p j q", q=128)),
                              writes=[R_tabB[kind]], dsem=D_tabB[kind])
                batch = [("call", load_tabs)]
                for i in range(8):
                    kind = 0 if i == 0 else 1 if i == 1 else 3 if i == 6 else 4 if i == 7 else 2
                    chunks = B_KINDS[kind][1] if kind != 2 else list(range(i + 1, i + 6))
                    off = B_KOFF[kind]
                    batch.append(dict(q=qTb[s_][:, i * 128:(i + 1) * 128], nq=128,
                                              kch=[kTb[s_][:, cc * 128:(cc + 1) * 128] for cc in chunks],
                                              vch=[Vb[s_][:, cc, :] for cc in chunks],
                                              tab=tabB[:, off:off + len(chunks), :], sink=None,
                                              out=out_bT[:, h, i * 128:(i + 1) * 128],
                                              regs_qk=[r_q, r_k], regs_v=[r_V], regs_tab=[R_tabB[kind]], regs_out=[R_ob]))
                for (col, chunks, toff) in ((1024, list(range(0, 5)), 0), (1025, list(range(9, 13)), 5)):
                    batch.append(dict(q=qTb[s_][:, col:col + 1], nq=1,
                                              kch=[kTb[s_][:, cc * 128:(cc + 1) * 128] for cc in chunks],
                                              vch=[Vb[s_][:, cc, :] for cc in chunks],
                                              tab=tabBx[:, h * 9 + toff:h * 9 + toff + len(chunks)].rearrange("p (j q) -> p j q", q=1),
                                              sink=None, out=out_bT[:, h, col:col + 1],
                                              regs_qk=[r_q, r_k], regs_v=[r_V], regs_tab=[reg("tabBx")], regs_out=[R_ob]))
                defer_units(1, batch)
            drain_units(2, atB)
            if p == 0:
                dump("ob", BIG[:, OB0:OB0 + 8208], [R_ob], [128, 8208], BF16)
            if stop_after == "C":
                break

            S.barrier()
            mg = carve(0, 16416).rearrange("p (k t) -> p k t", k=16)
            sga = [carve(16416, 1024, F32), carve(17440, 1024, F32)]
            sgb = [carve(18464, 1024, F32), carve(19488, 1024, F32)]
            m1 = [carve(20512, 1024, F32), carve(21536, 1024, F32)]
            m2 = [carve(22560, 1024, F32), carve(23584, 1024, F32)]
            for j in range(16):
                ws = wload2(48, 128, [(0, 16, 0, wsrc(w_in, 4608 + j * 128, 128)),
                                      (16, 32, 0, wsrc(w_in, 6656 + j * 128, 128)),
                                      (32, 40, 0, wsrc(w_ba, j * 128, 128)),
                                      (40, 48, 0, wsrc(w_bb, j * 128, 128))])
                w3 = wv(ws, 48, 128)
                for n in range(3):
                    nn = QN[n]
                    o0, o1 = QO[n]
                    t = (j * 3 + n) % 2
                    r_sga, r_sgb, r_m1, r_m2 = reg(f"sga{t}"), reg(f"sgb{t}"), reg(f"m1{t}"), reg(f"m2{t}")
                    b1 = next_bank()
                    for kc in range(16):
                        S.add("pe", i_mm(ps[:, b1, 0:nn], w3[:, kc, :], hq(kc, n), kc == 0, kc == 15),
                              reads=[R_w[ws]] + hq_regs(n), writes=[R_bank[b1]])
                    S.add("act", i_act(sga[t][:, 0:nn], ps[:, b1, 0:nn], AF.Sigmoid), reads=[R_bank[b1]], writes=[r_sga])
                    b2 = next_bank()
                    for kc in range(16):
                        S.add("pe", i_mm(ps[:, b2, 0:nn], w3[:, 16 + kc, :], hq(kc, n), kc == 0, kc == 15),
                              reads=[R_w[ws]] + hq_regs(n), writes=[R_bank[b2]])
                    S.add("act", i_act(sgb[t][:, 0:nn], ps[:, b2, 0:nn], AF.Sigmoid), reads=[R_bank[b2]], writes=[r_sgb])
                    b3 = next_bank()
                    for kc in range(8):
                        S.add("pe", i_mm(ps[:, b3, 0:nn], w3[:, 32 + kc, :], out_aT[:, kc, o0:o1], kc == 0, kc == 7),
                              reads=[R_w[ws], R_oa], writes=[R_bank[b3]])
                    S.add("dve", i_tt(m1[t][:, 0:nn], ps[:, b3, 0:nn], sga[t][:, 0:nn], ALU.mult),
                          reads=[R_bank[b3], r_sga], writes=[r_m1])
                    b4 = next_bank()
                    for kc in range(8):
                        S.add("pe", i_mm(ps[:, b4, 0:nn], w3[:, 40 + kc, :], out_bT[:, kc, o0:o1], kc == 0, kc == 7),
                              reads=[R_w[ws], R_ob], writes=[R_bank[b4]])
                    S.add("dve", i_tt(m2[t][:, 0:nn], ps[:, b4, 0:nn], sgb[t][:, 0:nn], ALU.mult),
                          reads=[R_bank[b4], r_sgb], writes=[r_m2])
                    S.add("dve", i_tt(mg[:, j, o0:o1], m1[t][:, 0:nn], m2[t][:, 0:nn], ALU.add),
                          reads=[r_m1, r_m2], writes=[R_mg[n]])
            if p == 0:
                dump("mg", carve(0, 16416), R_mg, [128, 16416], BF16)
            if stop_after == "D":
                break

            S.barrier()
            xt2 = [carve(16416, 4096, F32), carve(20512, 4096, F32)]
            x1t = [carve(24608, 4096, F32), carve(28704, 4096, F32)]
            xn2 = [BIG[:, OB0:OB0 + 2048], BIG[:, OB0 + 2048:OB0 + 4096]]
            gb2 = BIG[:, OB0 + 4096:OB0 + 8192].bitcast(F32)
            h2T = HT[:, 0:16416].rearrange("p (k t) -> p k t", k=16)
            S.add("sp", i_dma(gb2[:, :], gffn_d), writes=[reg("gb2")], dsem=D_gb2)
            wo = []
            for fg in range(3):
                ws = wload2(16, 512, [(0, 16, 0, wsrc(w_out, fg * 512, 512))])
                wo.append((wv(ws, 16, 512), R_w[ws]))
            w4 = BIG[:, OA0:OA0 + 8192].rearrange("p (k m) -> p k m", k=16)
            S.add("pool", i_dma(w4, wsrc(w_out, 1536, 512)), writes=[R_oa], dsem=D_oa)
            wo.append((w4, R_oa))
            prev = None
            for tt in range(10):
                if tt < 9:
                    s_ = tt % 2
                    rows = 128 if tt < 8 else 2
                    cols = (tt * 128, tt * 128 + 128) if tt < 8 else (1024, 1026)
                    r_xt2, r_x1t = reg(f"xt2_{s_}"), reg(f"x1t{s_}")
                    if tt < 8:
                        S.add("sp", i_dma(xt2[s_][:, :], xe[p, HL + tt * 128:HL + (tt + 1) * 128, :]),
                              writes=[r_xt2], dsem=D_xt2[s_])
                    else:
                        S.add("sp", i_dma(xt2[s_][0:1, :], xe[p, HL - 1:HL, :]), writes=[r_xt2], dsem=D_xt2[s_])
                        S.add("sp", i_dma(xt2[s_][1:2, :], xe[p, HL + T:HL + T + 1, :]), writes=[r_xt2],
                              dsem=D_xt2[s_], join=True)
                    for fg in range(4):
                        b = next_bank()
                        w3, rw = wo[fg]
                        for kc in range(16):
                            S.add("pe", i_mm(ps[0:rows, b, 0:512], mg[:, kc, cols[0]:cols[1]], w3[:, kc, :], kc == 0, kc == 15),
                                  reads=[rw, R_mg[min(tt // 4, 2)]], writes=[R_bank[b]])
                        S.add("dve", i_tt(x1t[s_][0:rows, fg * 512:(fg + 1) * 512], ps[0:rows, b, 0:512],
                                          xt2[s_][0:rows, fg * 512:(fg + 1) * 512], ALU.add),
                              reads=[R_bank[b], r_xt2], writes=[r_x1t])
                    if tt < 8:
                        S.add("sp", i_dma(x1scr[p * T + tt * 128:p * T + (tt + 1) * 128, :], x1t[s_][:, :]),
                              reads=[r_x1t], writes=[R_x1scr[p][tt]], dsem=D_x1st[s_])
                    norm_stage1(s_, x1t, xn2, None, gb2, rows, pfx="x1t", gbreg="gb2")
                if prev is not None:
                    ptt, ps_, prow, pcols = prev
                    norm_stage2(ps_, xn2, prow,
                                lambda half, pcols=pcols: h2T[:, half * 8:(half + 1) * 8, pcols[0]:pcols[1]],
                                [R_h2T[ptt]], ptt, pfx="x1t")
                prev = (tt, tt % 2, 128 if tt < 8 else 2, (tt * 128, tt * 128 + 128) if tt < 8 else (1024, 1026)) if tt < 9 else None
            if p == 0:
                dump("h2T", HT[:, 0:16416], R_h2T, [128, 16416], BF16)
            if stop_after == "E":
                break

            S.barrier()
            act_t = BIG[:, 0:45056].rearrange("p (j t) -> p j t", j=44)
            ubg = HT[:, 16416:18468].bitcast(F32)
            ubv = HT[:, 18468:20520].bitcast(F32)
            tg = HT[:, 20520:22568].bitcast(F32)
            tv = HT[:, 22568:24616].bitcast(F32)
            cw3 = cw[:, :].rearrange("p (f k) -> p f k", k=3)
            r_ubg, r_ubv, r_tg, r_tv = reg("ubg"), reg("ubv"), reg("tg"), reg("tv")

            def h2_regs(n):
                return R_h2T[0:4] if n == 0 else R_h2T[4:8] if n == 1 else [R_h2T[8]]

            for u in range(22):
                ws = wload2(16, 512, [(0, 16, 0, wsrc(w_up, 2 * u * 128, 256)),
                                      (0, 16, 256, wsrc(w_up, DFF + 2 * u * 128, 256))])
                for q in range(2):
                    j = 2 * u + q
                    for (coff, ub, r_ub, tt_, r_t, f) in ((q * 128, ubg, r_ubg, tg, r_tg, j),
                                                           (256 + q * 128, ubv, r_ubv, tv, r_tv, NFF + j)):
                        def cons_u(n, b, ub=ub, r_ub=r_ub):
                            if n == 2:
                                S.add("dve", i_tt(ub[:, 0:1026:1025], ps[:, b, 0:2], uflag[:, 2 * p:2 * p + 2], ALU.mult),
                                      reads=[R_bank[b], R_const], writes=[r_ub])
                                return
                            dst = ub[:, 1:513] if n == 0 else ub[:, 513:1025]
                            S.add("act", i_act(dst, ps[:, b, 0:QN[n]], AF.Copy), reads=[R_bank[b]], writes=[r_ub])
                        gemm_fm(ws, 16, 512, coff, 0, lambda kc, n: h2T[:, kc, QO[n][0]:QO[n][1]], h2_regs,
                                [(0, 512), (1, 512), (2, 2)], cons_u)
                        S.add("dve", i_ts(tt_[:, :], ub[:, 1:1025], cw3[:, f, 1:2], cb[:, f:f + 1], ALU.mult, ALU.add),
                              reads=[r_ub, R_const], writes=[r_t])
                        S.add("dve", i_stt(tt_[:, :], ub[:, 0:1024], cw3[:, f, 0:1], tt_[:, :], ALU.mult, ALU.add),
                              reads=[r_ub, r_t, R_const], writes=[r_t])
                        S.add("dve", i_stt(tt_[:, :], ub[:, 2:1026], cw3[:, f, 2:3], tt_[:, :], ALU.mult, ALU.add),
                              reads=[r_ub, r_t, R_const], writes=[r_t])
                    S.add("act", i_act(tg[:, :], tg[:, :], AF.Silu), reads=[r_tg], writes=[r_tg])
                    S.add("dve", i_tt(act_t[:, j, :], tg[:, :], tv[:, :], ALU.mult), reads=[r_tg, r_tv], writes=[R_act])
            if p == 0:
                dump("act", BIG[:, 0:45056], [R_act], [128, 45056], BF16)
            if stop_after == "F":
                break

            S.barrier()
            x1s = [HT[:, i * 1024:(i + 1) * 1024].bitcast(F32) for i in range(8)]
            yst = [HT[:, 8192 + i * 1024:8192 + (i + 1) * 1024].bitcast(F32) for i in range(4)]
            for fg in range(4):
                for tt in range(8):
                    S.add("sp", i_dma(x1s[tt][:, :], x1scr[p * T + tt * 128:p * T + (tt + 1) * 128, fg * 512:(fg + 1) * 512]),
                          reads=[R_x1scr[p][tt]], writes=[reg(f"x1s{tt}")], dsem=D_x1s[tt])
                for ku in range(4):
                    ws = wload2(11, 512, [(0, 11, 0, w_down[ku * 1408:(ku + 1) * 1408, fg * 512:(fg + 1) * 512]
                                           .rearrange("(k p) m -> p k m", p=128))])
                    w3 = wv(ws, 11, 512)
                    for tt in range(8):
                        for kc in range(11):
                            S.add("pe", i_mm(ps[:, tt, :], act_t[:, ku * 11 + kc, tt * 128:(tt + 1) * 128], w3[:, kc, :],
                                             ku == 0 and kc == 0, ku == 3 and kc == 10),
                                  reads=[R_w[ws], R_act], writes=[R_bank[tt]])
                for tt in range(8):
                    r = (fg * 8 + tt) % 4
                    S.add("dve", i_tt(yst[r][:, :], ps[:, tt, :], x1s[tt][:, :], ALU.add),
                          reads=[R_bank[tt], reg(f"x1s{tt}")], writes=[reg(f"yst{r}")])
                    final_ops.append(S.add("sp", i_dma(out_d[p * T + tt * 128:p * T + (tt + 1) * 128, fg * 512:(fg + 1) * 512],
                                                       yst[r][:, :]),
                                           reads=[reg(f"yst{r}")], dsem=D_yst[r]))

        S.barrier()
        S.final_wait("sp", final_ops)
        S.emit()
    return nc, dbg_out


def _rope_tables(t0):
    pos = (t0 - HL + np.arange(NKV)).astype(np.float32)
    inv_freq = (np.float32(10000.0) ** (-np.arange(64, dtype=np.float32) * np.float32(2.0 / 128))).astype(np.float32)
    ang = pos[:, None] * inv_freq[None, :]
    cos = np.cos(ang).astype(np.float32)
    sin = np.sin(ang).astype(np.float32)
    cosT = np.concatenate([cos, cos], axis=1).T.copy()
    sinT = np.concatenate([sin, sin], axis=1).T.copy()
    return cosT, sinT


def _maskA(t0):
    out = np.full((128, 3, 3, 128), NEG, np.float32)
    k = np.arange(128)[:, None]
    q = np.arange(128)[None, :]
    for kind, i in enumerate((0, 3, 7)):
        c = 3 + i
        for j in range(3):
            kpos = t0 - HL + (c - 1 + j) * 128 + k
            qpos = t0 + i * 128 + q
            valid = (np.abs(qpos - kpos) <= 128) & (kpos >= 0) & (kpos < S_LEN)
            out[:, kind, j, :] = np.where(valid, 0.0, NEG / SCALE)
    return out.reshape(128, 9 * 128)


def _b_tab(rpb_h, qr, qc, kr, kc):
    qr_, qc_ = qr[None, :], qc[None, :]
    kr_, kc_ = kr[:, None], kc[:, None]
    rs = np.clip(qr_ - 4, 0, 64 - 8)
    cs = np.clip(qc_ - 8, 0, 64 - 16)
    valid = (kr_ >= rs) & (kr_ < rs + 8) & (kc_ >= cs) & (kc_ < cs + 16) & (kr_ >= 0) & (kr_ < 64)
    ir = np.clip(kr_ - qr_ + 7, 0, 14)
    ic = np.clip(kc_ - qc_ + 15, 0, 30)
    return np.where(valid, rpb_h[ir, ic], np.float32(NEG)).astype(np.float32)


B_KINDS = [(0, list(range(1, 7))), (1, list(range(2, 7))), (2, list(range(3, 8))),
           (6, list(range(7, 12))), (7, list(range(7, 13)))]
B_KOFF = [0, 6, 11, 16, 21]


def _tabB(rpb, t0):
    r0 = t0 // 64
    tab = np.empty((8, 128, 27, 128), np.float32)
    kk = np.arange(128)
    for h in range(8):
        for (i, chunks), off in zip(B_KINDS, B_KOFF):
            qr = r0 + 2 * i + kk // 64
            qc = kk % 64
            for j, cc in enumerate(chunks):
                kr = r0 - 6 + 2 * cc + kk // 64
                kc = kk % 64
                tab[h, :, off + j, :] = _b_tab(rpb[h], qr, qc, kr, kc)
    tabx = np.empty((128, 8, 9), np.float32)
    for h in range(8):
        for j, cc in enumerate(range(0, 5)):
            kr = r0 - 6 + 2 * cc + kk // 64
            tabx[:, h, j] = _b_tab(rpb[h], np.array([r0 - 1]), np.array([63]), kr, kk % 64)[:, 0]
        for j, cc in enumerate(range(9, 13)):
            kr = r0 - 6 + 2 * cc + kk // 64
            tabx[:, h, 5 + j] = _b_tab(rpb[h], np.array([r0 + 16]), np.array([0]), kr, kk % 64)[:, 0]
    return tab.reshape(8, 128, 27 * 128), tabx.reshape(128, 72)


def host_prepare(inputs, cores=range(8)):
    f = lambda a: np.ascontiguousarray(np.asarray(a, dtype=np.float32))
    x = f(inputs["x"])
    shared = {
        "w_in": f(inputs["w_in"][0]), "w_ba": f(inputs["w_branch_a"][0]), "w_bb": f(inputs["w_branch_b"][0]),
        "w_out": f(inputs["w_out"][0]), "w_up": f(inputs["w_up"][0]), "w_down": f(inputs["w_down"][0]),
        "gmix_b": np.ascontiguousarray(np.broadcast_to(f(inputs["norm_mix"][0])[None, :], (128, D))),
        "gffn_b": np.ascontiguousarray(np.broadcast_to(f(inputs["norm_ffn"][0])[None, :], (128, D))),
        "gains": np.ascontiguousarray(np.stack([f(inputs["a_q_norm"][0]), f(inputs["a_k_norm"][0]),
                                                f(inputs["b_q_norm"][0]), f(inputs["b_k_norm"][0])], axis=1)),
        "sink_b": np.ascontiguousarray(np.broadcast_to(f(inputs["a_sink"][0])[None, :], (128, 8))),
        "cw_fm": np.ascontiguousarray(f(inputs["conv_w"][0]).T.reshape(88, 128, 3).transpose(1, 0, 2).reshape(128, 264)),
        "cb_fm": np.ascontiguousarray(f(inputs["conv_b"][0]).reshape(88, 128).T),
        "ident": np.eye(128, dtype=np.float32),
    }
    rm = np.zeros((128, 128), np.float32)
    for m in range(64):
        rm[m + 64, m] = -1.0
        rm[m, m + 64] = 1.0
    shared["rm"] = rm
    rpb = f(inputs["b_rpb"][0])
    per_t0 = {}
    for t0 in range(0, S_LEN, T):
        cosT, sinT = _rope_tables(t0)
        tb, tbx = _tabB(rpb, t0)
        per_t0[t0] = (cosT, sinT, _maskA(t0), tb, tbx)
    in_maps = []
    for c in cores:
        b, half = c // 2, c % 2
        m = dict(shared)
        xes, cs, sn, ma, tb, tbx = [], [], [], [], [], []
        for p in range(NPASS):
            t0 = half * TOK_CORE + p * T
            lo, hi = t0 - HL, t0 - HL + NKV
            xp = np.zeros((NKV, D), np.float32)
            a, bb = max(lo, 0), min(hi, S_LEN)
            xp[a - lo:bb - lo] = x[b, a:bb]
            xes.append(xp)
            t = per_t0[t0]
            cs.append(t[0]); sn.append(t[1]); ma.append(t[2]); tb.append(t[3]); tbx.append(t[4])
        m["xe"] = np.stack(xes)
        m["cos_t"] = np.stack(cs)
        m["sin_t"] = np.stack(sn)
        m["maskA"] = np.stack(ma)
        m["tabB"] = np.stack(tb)
        m["tabBx"] = np.stack(tbx)
        fl = np.zeros((128, 2 * NPASS), np.float32)
        for p in range(NPASS):
            t0 = half * TOK_CORE + p * T
            fl[:, 2 * p] = 1.0 if t0 - 1 >= 0 else 0.0
            fl[:, 2 * p + 1] = 1.0 if t0 + T < S_LEN else 0.0
        m["uflag"] = fl
        in_maps.append(m)
    return in_maps


_CACHE = {}


def kernel(**inputs):
    in_maps = host_prepare(inputs)
    if "nc" not in _CACHE:
        _CACHE["nc"] = build_program()[0]
    nc = _CACHE["nc"]
    res = run_bass_kernel_spmd(nc, in_maps, core_ids=list(range(8)))
    out = np.empty((4, S_LEN, D), np.float32)
    for c in range(8):
        b, half = c // 2, c % 2
        out[b, half * TOK_CORE:(half + 1) * TOK_CORE] = res.results[c]["out"]
    return out
```

```python
import contextlib
import numpy as np
import concourse.bass as bass
import concourse.mybir as mybir
from concourse.bass_utils import run_bass_kernel_spmd

F32 = mybir.dt.float32
BF16 = mybir.dt.bfloat16
ALU = mybir.AluOpType
AF = mybir.ActivationFunctionType

D = 2048
S_LEN = 4096
T = 1024
NCH = 13
NKV = NCH * 128
HL = 384
NQ = T + 2
DFF = 5632
NFF = DFF // 128
INC = 8704
EPS = 1e-6
NEG = -30000.0
SCALE = 1.0 / float(np.sqrt(128.0))
NPASS = 2
TOK_CORE = NPASS * T

ENGS = ("pe", "act", "dve", "pool", "sp")
CAP = 30000


class Region:
    __slots__ = ("name", "writers", "readers")

    def __init__(self, name):
        self.name = name
        self.writers = {}
        self.readers = {}


class DSem:
    __slots__ = ("name", "count", "sem")

    def __init__(self, name):
        self.name = name
        self.count = 0
        self.sem = None


class Op:
    __slots__ = ("eng", "fn", "deps", "dsem", "dval", "sigidx", "key", "signal")

    def __init__(self, eng, fn, dsem):
        self.eng = eng
        self.fn = fn
        self.deps = set()
        self.dsem = dsem
        self.dval = 0
        self.sigidx = 0
        self.signal = False
        self.key = eng if dsem is None else ("dma", id(self))


class Sched:
    def __init__(self, nc):
        self.nc = nc
        self.ops = {e: [] for e in ENGS}
        self.pending = {e: set() for e in ENGS}
        self.dsems = []
        self.dma_live = []
        self.last = {}
        self.final = {e: set() for e in ENGS}

    def dsem(self, name):
        d = DSem(name)
        self.dsems.append(d)
        return d

    def add(self, eng, fn, reads=(), writes=(), dsem=None, join=False):
        op = Op(eng, fn, dsem)
        deps = set(self.pending[eng])
        self.pending[eng] = set()
        is_dma = dsem is not None
        for r in reads:
            for k, w in r.writers.items():
                if eng == "pe" and k == "pe":
                    continue
                deps.add(w)
        if not join:
            for r in writes:
                for k, w in r.writers.items():
                    if eng == "pe" and k == "pe":
                        continue
                    deps.add(w)
                for k, w in r.readers.items():
                    if eng == "pe" and k == "pe":
                        continue
                    deps.add(w)
        op.deps = deps
        for r in writes:
            if join:
                r.writers[op.key] = op
            else:
                r.writers = {op.key: op}
                r.readers = {}
        for r in reads:
            if r in writes:
                continue
            r.readers[op.key] = op
        if is_dma:
            dsem.count += 16
            op.dval = dsem.count
            self.dma_live.append(op)
        else:
            self.last[eng] = op
        self.ops[eng].append(op)
        return op

    def barrier(self, engines=("pe", "act", "dve", "sp")):
        deps = set()
        for e in engines:
            if e in self.last:
                deps.add(self.last[e])
        keep = []
        for op in self.dma_live:
            if op.eng in engines:
                deps.add(op)
            else:
                keep.append(op)
        self.dma_live = keep
        for e in engines:
            self.pending[e] |= {d for d in deps if d.key != e}

    def final_wait(self, eng, ops):
        self.final[eng] |= set(ops)

    def emit(self):
        nc = self.nc
        needed = set()
        for e in ENGS:
            for op in self.ops[e]:
                needed |= op.deps
            needed |= self.final[e]
        counts = {}
        for e in ENGS:
            c = 0
            for op in self.ops[e]:
                if op.dsem is None and op in needed:
                    c += 1
                    op.sigidx = c
                    op.signal = True
            counts[e] = c
        with contextlib.ExitStack() as st:
            esems = {}
            for e in ENGS:
                n = (counts[e] + CAP - 1) // CAP
                esems[e] = [st.enter_context(nc.semaphore(f"c_{e}_{i}")) for i in range(n)]
            for d in self.dsems:
                if d.count > 0:
                    d.sem = st.enter_context(nc.semaphore(f"d_{d.name}"))

            def resolve(op):
                if op.dsem is not None:
                    return op.dsem.sem, op.dval
                i = (op.sigidx - 1) // CAP
                return esems[op.eng][i], op.sigidx - i * CAP

            block = st.enter_context(nc.Block())
            sched = self

            def make_body(e):
                def body(eng):
                    waited = {}

                    def do_wait(d):
                        sem, val = resolve(d)
                        k = sem.num
                        if waited.get(k, 0) < val:
                            eng.wait_ge(sem, val)
                            waited[k] = val

                    def wait_all(deps):
                        best = {}
                        for d in deps:
                            sem, val = resolve(d)
                            if val > best.get(sem.num, (None, 0))[1]:
                                best[sem.num] = (sem, val)
                        for k in sorted(best):
                            sem, val = best[k]
                            if waited.get(k, 0) < val:
                                eng.wait_ge(sem, val)
                                waited[k] = val

                    for op in sched.ops[e]:
                        wait_all(op.deps)
                        ins = op.fn(eng)
                        if op.dsem is not None:
                            ins.then_inc(op.dsem.sem, 16)
                        elif op.signal:
                            sem, _ = resolve(op)
                            ins.then_inc(sem, 1)
                    wait_all(sched.final[e])
                return body

            hw = {"pe": block.tensor, "act": block.scalar, "dve": block.vector,
                  "pool": block.gpsimd, "sp": block.sync}
            for e in ENGS:
                if self.ops[e] or self.final[e]:
                    hw[e](make_body(e))
        return counts


def i_mm(out, lhsT, rhs, start, stop):
    return lambda e: e.matmul(out, lhsT=lhsT, rhs=rhs, start=start, stop=stop)


def i_tr(out, in_, ident):
    return lambda e: e.transpose(out=out, in_=in_, identity=ident)


def i_act(out, in_, func, scale=None, accum_out=None, bias=None):
    kw = {}
    if bias is not None:
        kw["bias"] = bias
    if scale is not None:
        kw["scale"] = scale
    if accum_out is not None:
        kw["accum_out"] = accum_out
    return lambda e: e.activation(out=out, in_=in_, func=func, **kw)


def i_ts(out, in0, s1, s2, op0, op1=None):
    if op1 is None:
        return lambda e: e.tensor_scalar(out=out, in0=in0, scalar1=s1, scalar2=None, op0=op0)
    return lambda e: e.tensor_scalar(out=out, in0=in0, scalar1=s1, scalar2=s2, op0=op0, op1=op1)


def i_stt(out, in0, scalar, in1, op0, op1):
    return lambda e: e.scalar_tensor_tensor(out=out, in0=in0, scalar=scalar, in1=in1, op0=op0, op1=op1)


def i_tt(out, in0, in1, op):
    return lambda e: e.tensor_tensor(out=out, in0=in0, in1=in1, op=op)


def i_vcopy(out, in_):
    return lambda e: e.tensor_copy(out=out, in_=in_)


def i_recip(out, in_):
    return lambda e: e.reciprocal(out=out, in_=in_)


def i_dma(out, in_):
    return lambda e: e.dma_start(out=out, in_=in_)


def build_program(npass=NPASS, stop_after=None, debug=()):
    nc = bass.Bass("TRN2", target_bir_lowering=False)
    S = Sched(nc)

    def din(name, shape, dt=F32):
        return nc.dram_tensor(name, list(shape), dt, kind="ExternalInput").ap()

    xe = din("xe", [NPASS, NKV, D])
    w_in = din("w_in", [D, INC])
    w_ba = din("w_ba", [1024, D])
    w_bb = din("w_bb", [1024, D])
    w_out = din("w_out", [D, D])
    w_up = din("w_up", [D, 2 * DFF])
    w_down = din("w_down", [DFF, D])
    gmix_d = din("gmix_b", [128, D])
    gffn_d = din("gffn_b", [128, D])
    gains_d = din("gains", [128, 4])
    gmixfm_d = din("gmix_fm", [128, 16])
    sink_d = din("sink_b", [128, 8])
    cw_d = din("cw_fm", [128, 88 * 3])
    cb_d = din("cb_fm", [128, 88])
    ident_d = din("ident", [128, 128])
    rm_d = din("rm", [128, 128])
    cos_d = din("cos_t", [NPASS, 128, NKV])
    sin_d = din("sin_t", [NPASS, 128, NKV])
    maskA_d = din("maskA", [NPASS, 128, 9 * 128])
    tabB_d = din("tabB", [NPASS, 8, 128, 27 * 128])
    tabBx_d = din("tabBx", [NPASS, 128, 72])
    uflag_d = din("uflag", [128, 2 * NPASS])
    out_d = nc.dram_tensor("out", [TOK_CORE, D], F32, kind="ExternalOutput").ap()
    x1scr = nc.dram_tensor("x1scr", [TOK_CORE, D], F32).ap()
    dbg_out = {}

    st = contextlib.ExitStack()
    with st:
        def sb(name, shape, dt):
            return st.enter_context(nc.sbuf_tensor("sb_" + name, list(shape), dt))

        ident = sb("ident", [128, 128], BF16)
        ident_f = sb("ident_f", [128, 128], F32)
        ones_f = sb("ones_f", [128, 128], F32)
        ones_b = sb("ones_b", [128, 128], BF16)
        ones_ms = sb("ones_ms", [128, 128], BF16)
        rm_f = sb("rm_f", [128, 128], F32)
        rm_b = sb("rm_b", [128, 128], BF16)
        gains = sb("gains", [128, 4], F32)
        gmixfm = sb("gmixfm", [128, 16], F32)
        esink = sb("esink", [128, 8], F32)
        cw = sb("cw", [128, 88 * 3], F32)
        cb = sb("cb", [128, 88], F32)
        tabBx = sb("tabBx", [128, 72], F32)
        ssb = sb("ssb", [128, 4], F32)
        rstdb = sb("rstdb", [128, 4], F32)
        epsc = sb("epsc", [128, 1], F32)
        uflag = sb("uflag", [128, 2 * NPASS], F32)
        HT = sb("HT", [128, 16 * NKV], BF16)
        WR = [sb(f"WR{i}", [128, 8192], BF16) for i in range(3)]
        NBIG = 51000
        BIG = sb("BIG", [128, NBIG], BF16)
        ps = st.enter_context(nc.psum_tensor("ps", [128, 8, 512], F32))
        psb = ps.bitcast(BF16)

        OA0 = NBIG - 2 * 8208
        OB0 = NBIG - 8208

        def carve(off, n, dt=BF16):
            v = BIG[:, off:off + n]
            return v.bitcast(F32) if dt == F32 else v

        hT3 = HT[:, :].rearrange("p (k t) -> p k t", k=16)
        out_aT = BIG[:, OA0:OA0 + 8208].rearrange("p (h t) -> p h t", h=8)
        out_bT = BIG[:, OB0:OB0 + 8208].rearrange("p (h t) -> p h t", h=8)

        R = {}

        def reg(name):
            if name not in R:
                R[name] = Region(name)
            return R[name]

        R_bank = [reg(f"bank{b}") for b in range(8)]
        R_w = [reg(f"w{i}") for i in range(3)]
        D_w = [S.dsem(f"w{i}") for i in range(3)]
        R_hT = [reg(f"hT{c}") for c in range(NCH)]
        R_const = reg("const")
        D_const = S.dsem("const")
        bank_ctr = [0]

        held = set()

        def next_bank(hold=False):
            while True:
                b = bank_ctr[0] % 8
                bank_ctr[0] += 1
                if b not in held:
                    break
            if hold:
                held.add(b)
            return b

        wctr = [0]

        def wload(parts):
            s = wctr[0] % 3
            wctr[0] += 1
            for i, (dst, src) in enumerate(parts):
                S.add("pool", i_dma(dst(WR[s]), src), writes=[R_w[s]], dsem=D_w[s], join=(i > 0))
            return s

        def wv(s, nk, ncol):
            return WR[s][:, 0:nk * ncol].rearrange("p (k m) -> p k m", k=nk)

        def cload(dst, src):
            S.add("sp", i_dma(dst, src), writes=[R_const], dsem=D_const, join=True)

        cload(ident_f[:], ident_d)
        cload(rm_f[:], rm_d)
        cload(gains[:], gains_d)
        cload(gmixfm[:], gmixfm_d)
        cload(esink[:], sink_d)
        cload(cw[:], cw_d)
        cload(cb[:], cb_d)
        cload(uflag[:], uflag_d)
        S.barrier()
        S.add("dve", i_vcopy(ident[:], ident_f[:]), reads=[R_const], writes=[R_const])
        S.add("dve", i_vcopy(rm_b[:], rm_f[:]), writes=[R_const])
        S.add("dve", lambda e: e.memset(ones_f[:], 1.0 / 128.0), writes=[R_const])
        S.add("dve", lambda e: e.memset(ones_b[:], 1.0), writes=[R_const])
        S.add("dve", lambda e: e.memset(ones_ms[:], 1.0 / 128.0), writes=[R_const])
        S.add("dve", lambda e: e.memset(epsc[:], EPS), writes=[R_const])
        S.add("act", i_act(esink[:], esink[:], AF.Exp), reads=[R_const], writes=[R_const])
        S.barrier()

        qcols_h = [(HL, HL + 512), (HL + 512, HL + 1024)]

        def hq(kc, n):
            if n < 2:
                return hT3[:, kc, qcols_h[n][0]:qcols_h[n][1]]
            return hT3[:, kc, HL - 1:HL + T + 1:T + 1]

        def hq_regs(n):
            if n == 0:
                return R_hT[3:7]
            if n == 1:
                return R_hT[7:11]
            return [R_hT[2], R_hT[11]]

        QN = [512, 512, 2]
        QO = [(0, 512), (512, 1024), (1024, 1026)]
        KVN = [(0, 512), (512, 1024), (1024, 1536), (1536, 1664)]

        def kv_regs(n):
            lo, hi = KVN[n]
            return R_hT[lo // 128:(hi + 127) // 128]

        def tcol(ap, n):
            if n < 2:
                return ap[:, qcols_h[n][0]:qcols_h[n][1]]
            return ap[:, HL - 1:HL + T + 1:T + 1]

        final_ops = []

        def norm_stage1(slot, xt, xn, junk, gbt, rows, pfx="xt", gbreg="gb"):
            r_xt, r_xn = reg(f"{pfx}{slot}"), reg(f"{pfx}n{slot}")
            r_ss, r_rs = reg(f"ss{slot}"), reg(f"rs{slot}")
            jk = junk if junk is not None else xn[slot]
            jr = reg("junk") if junk is not None else r_xn
            S.add("act", i_act(jk[0:rows, :], xt[slot][0:rows, :], AF.Square,
                               accum_out=ssb[0:rows, slot:slot + 1]),
                  reads=[r_xt], writes=[jr, r_ss])
            S.add("act", i_act(rstdb[0:rows, slot:slot + 1], ssb[0:rows, slot:slot + 1], AF.Ln,
                               scale=1.0 / D, bias=epsc[0:rows, 0:1]), reads=[r_ss, R_const], writes=[r_rs])
            S.add("act", i_act(rstdb[0:rows, slot:slot + 1], rstdb[0:rows, slot:slot + 1], AF.Exp, scale=-0.5),
                  reads=[r_rs], writes=[r_rs])
            S.add("dve", i_stt(xn[slot][0:rows, :], xt[slot][0:rows, :], rstdb[0:rows, slot:slot + 1],
                               gbt[0:rows, :], ALU.mult, ALU.mult),
                  reads=[r_xt, r_rs, reg(gbreg)], writes=[r_xn])

        def norm_stage2(slot, xn, rows, dst_fn, dst_regs, ei, pfx="xt"):
            r_xn = reg(f"{pfx}n{slot}")
            for half in range(2):
                b = next_bank()
                for k in range(8):
                    kk = half * 8 + k
                    S.add("pe", i_tr(psb[:, b, k * 128:k * 128 + rows], xn[slot][0:rows, kk * 128:(kk + 1) * 128],
                                     ident[0:rows, 0:rows]),
                          reads=[r_xn, R_const], writes=[R_bank[b]])
                src = psb[:, b, 0:1024].rearrange("p (k t) -> p k t", k=8)[:, :, 0:rows]
                eng = "act" if (ei + half) % 2 == 0 else "dve"
                if eng == "act":
                    S.add("act", i_act(dst_fn(half), src, AF.Copy), reads=[R_bank[b]], writes=dst_regs)
                else:
                    S.add("dve", i_vcopy(dst_fn(half), src), reads=[R_bank[b]], writes=dst_regs)


        def wsrc(w, c0, n):
            return w[:, c0:c0 + n].rearrange("(k p) m -> p k m", p=128)

        def wload2(nk, ncol, parts):
            s_ = wctr[0] % 3
            wctr[0] += 1
            w3 = wv(s_, nk, ncol)
            for i, (k0, k1, c0, src) in enumerate(parts):
                n = src.shape[-1]
                S.add("pool", i_dma(w3[:, k0:k1, c0:c0 + n], src), writes=[R_w[s_]], dsem=D_w[s_], join=(i > 0))
            return s_

        def gemm_fm(ws, nk, wcols, coff, k0, rhs_fn, rhs_regs_fn, nlist, consume):
            w3 = wv(ws, nk, wcols)
            nkk = nlist[0][2] if len(nlist[0]) > 2 else None
            for (n, nn) in nlist:
                b = next_bank()
                cnt = 16
                for kc in range(cnt):
                    S.add("pe", i_mm(ps[:, b, 0:nn], w3[:, k0 + kc, coff:coff + 128], rhs_fn(kc, n), kc == 0, kc == cnt - 1),
                          reads=[R_w[ws]] + rhs_regs_fn(n), writes=[R_bank[b]])
                consume(n, b)

        def mk_tmp(offs, pfx):
            d = {}
            for nm, (o, n, dt) in offs.items():
                d[nm] = carve(o, n, dt)
                d["r_" + nm] = reg(pfx + nm)
            return d

        class Pipe:
            def __init__(self):
                self.live = []

            def push(self, gen):
                try:
                    next(gen)
                    self.live.append(gen)
                except StopIteration:
                    pass

            def tick(self):
                nl = []
                for g_ in self.live:
                    try:
                        next(g_)
                        nl.append(g_)
                    except StopIteration:
                        pass
                self.live = nl

            def flush(self):
                while self.live:
                    self.tick()

        pipe = Pipe()
        pending_units = []
        unit_ctr = [0]

        round_ctr = [0]
        deferred = []

        def defer_units(delay, items):
            deferred.append((round_ctr[0] + delay, items))

        def round_end(new_gen=None, nunits=0, tmps=None):
            pipe.tick()
            round_ctr[0] += 1
            while deferred and deferred[0][0] <= round_ctr[0]:
                pending_units.extend(deferred.pop(0)[1])
            if new_gen is not None:
                pipe.push(new_gen)
            k = 0
            while k < nunits and pending_units:
                u = pending_units.pop(0)
                if isinstance(u, tuple):
                    u[1]()
                    continue
                t = tmps[unit_ctr[0] % len(tmps)]
                unit_ctr[0] += 1
                pipe.push(unit_gen(u, t))
                k += 1

        def drain_units(nunits, tmps):
            while deferred or pending_units:
                round_end(None, nunits=nunits, tmps=tmps)
            pipe.flush()

        def gemm_tile(ws, nk, wcols, coff, rhs_fn, rhs_regs, nn):
            w3 = wv(ws, nk, wcols)
            b = next_bank(hold=True)
            for kc in range(16):
                S.add("pe", i_mm(ps[:, b, 0:nn], w3[:, kc, coff:coff + 128], rhs_fn(kc), kc == 0, kc == 15),
                      reads=[R_w[ws]] + rhs_regs, writes=[R_bank[b]])
            return b

        def qknorm_gen(b, n, gain_ap, out_ap, out_regs, tmp, rope=None):
            psv = ps[:, b, 0:n]
            sqb = tmp["sq"].bitcast(BF16)
            S.add("act", i_act(sqb[:, 0:n], psv, AF.Square), reads=[R_bank[b]], writes=[tmp["r_sq"]])
            yield
            b2 = next_bank()
            S.add("pe", i_mm(ps[:, b2, 0:n], ones_ms[:, :], sqb[:, 0:n], True, True),
                  reads=[tmp["r_sq"], R_const], writes=[R_bank[b2]])
            S.add("act", i_act(tmp["rstd"][:, 0:n], ps[:, b2, 0:n], AF.Ln, bias=epsc[:, 0:1]),
                  reads=[R_bank[b2], R_const], writes=[tmp["r_rstd"]])
            S.add("act", i_act(tmp["rstd"][:, 0:n], tmp["rstd"][:, 0:n], AF.Exp, scale=-0.5),
                  reads=[tmp["r_rstd"]], writes=[tmp["r_rstd"]])
            if rope is None:
                S.add("dve", i_stt(out_ap, psv, gain_ap, tmp["rstd"][:, 0:n], ALU.mult, ALU.mult),
                      reads=[R_bank[b], tmp["r_rstd"], R_const], writes=out_regs)
                held.discard(b)
                return
            cos_ap, sin_ap = rope
            y32, t1, t2 = tmp["y32"], tmp["t1"], tmp["t2"]
            S.add("dve", i_stt(y32[:, 0:n], psv, gain_ap, tmp["rstd"][:, 0:n], ALU.mult, ALU.mult),
                  reads=[R_bank[b], tmp["r_rstd"], R_const], writes=[tmp["r_y32"]])
            held.discard(b)
            S.add("pool", i_tt(t1[:, 0:n], y32[:, 0:n], cos_ap, ALU.mult),
                  reads=[tmp["r_y32"], reg("rope")], writes=[tmp["r_t1"]])
            S.add("dve", i_tt(t2[0:64, 0:n], y32[64:128, 0:n], sin_ap[64:128], ALU.mult),
                  reads=[tmp["r_y32"], reg("rope")], writes=[tmp["r_t2"]])
            S.add("dve", i_tt(t2[64:128, 0:n], y32[0:64, 0:n], sin_ap[0:64], ALU.mult),
                  reads=[tmp["r_y32"], reg("rope")], writes=[tmp["r_t2"]])
            S.add("pool", i_tt(out_ap, t1[:, 0:n], t2[:, 0:n], ALU.add),
                  reads=[tmp["r_t1"], tmp["r_t2"]], writes=out_regs)

        def vtr_gen(vT, r_vT, Vdst, r_V):
            yield
            for gi, (c0, c1) in enumerate(((0, 8), (8, 13))):
                b = next_bank()
                for c in range(c0, c1):
                    S.add("pe", i_tr(psb[:, b, (c - c0) * 128:(c - c0 + 1) * 128], vT[:, c * 128:(c + 1) * 128], ident[:, :]),
                          reads=[r_vT, R_const], writes=[R_bank[b]])
                src = psb[:, b, 0:(c1 - c0) * 128].rearrange("p (c d) -> p c d", c=c1 - c0)
                if gi == 0:
                    S.add("dve", i_vcopy(Vdst[:, c0:c1, :], src), reads=[R_bank[b]], writes=[r_V])
                    yield
                else:
                    S.add("act", i_act(Vdst[:, c0:c1, :], src, AF.Copy), reads=[R_bank[b]], writes=[r_V])

        def unit_gen(u, t):
            attn_s1(u, t)
            yield
            attn_s2(u, t)

        def attn_s1(u, t):
            nch, nq = len(u["kch"]), u["nq"]
            P = t["P"].rearrange("p (j q) -> p j q", q=128)
            if u.get("mask_pe") is not None:
                b = next_bank()
                for j in range(nch):
                    S.add("pe", i_mm(ps[:, b, j * nq:(j + 1) * nq], u["kch"][j], u["q"], True, False),
                          reads=u["regs_qk"], writes=[R_bank[b]])
                    S.add("pe", i_mm(ps[:, b, j * nq:(j + 1) * nq], ident[:, :], u["mask_pe"][j], False, True),
                          reads=u["regs_tab"] + [R_const], writes=[R_bank[b]])
                src = ps[:, b, 0:nch * nq].rearrange("p (j q) -> p j q", j=nch)
                S.add("act", i_act(P[:, 0:nch, 0:nq], src, AF.Exp, scale=SCALE), reads=[R_bank[b]], writes=[t["r_P"]])
                return
            E = t["E"].rearrange("p (j q) -> p j q", q=128)
            for g0 in range(0, nch, 4):
                g1 = min(nch, g0 + 4)
                b = next_bank()
                for j in range(g0, g1):
                    S.add("pe", i_mm(ps[:, b, (j - g0) * nq:(j - g0 + 1) * nq], u["kch"][j], u["q"], True, True),
                          reads=u["regs_qk"], writes=[R_bank[b]])
                src = ps[:, b, 0:(g1 - g0) * nq].rearrange("p (j q) -> p j q", j=g1 - g0)
                S.add("dve", i_stt(E[:, g0:g1, 0:nq], src, SCALE, u["tab"][:, g0:g1, :], ALU.mult, ALU.add),
                      reads=[R_bank[b]] + u["regs_tab"], writes=[t["r_E"]])
            S.add("act", i_act(P[:, 0:nch, 0:nq], E[:, 0:nch, 0:nq], AF.Exp), reads=[t["r_E"]], writes=[t["r_P"]])

        def attn_s2(u, t):
            nch, nq = len(u["kch"]), u["nq"]
            P = t["P"].rearrange("p (j q) -> p j q", q=128)
            rz = t["rz"]
            b = next_bank()
            for j in range(nch):
                S.add("pe", i_mm(ps[:, b, 0:nq], u["vch"][j], P[:, j, 0:nq], j == 0, j == nch - 1),
                      reads=[t["r_P"]] + u["regs_v"], writes=[R_bank[b]])
            for j in range(nch):
                S.add("pe", i_mm(ps[:, b, 128:128 + nq], ones_b[:, :], P[:, j, 0:nq], j == 0, j == nch - 1),
                      reads=[t["r_P"], R_const], writes=[R_bank[b]])
            if u["sink"] is not None:
                S.add("dve", i_ts(rz[:, 0:nq], ps[:, b, 128:128 + nq], u["sink"], None, ALU.add),
                      reads=[R_bank[b], R_const], writes=[t["r_rz"]])
                S.add("dve", i_recip(rz[:, 0:nq], rz[:, 0:nq]), reads=[t["r_rz"]], writes=[t["r_rz"]])
            else:
                S.add("dve", i_recip(rz[:, 0:nq], ps[:, b, 128:128 + nq]), reads=[R_bank[b]], writes=[t["r_rz"]])
            S.add("dve", i_tt(u["out"], ps[:, b, 0:nq], rz[:, 0:nq], ALU.mult),
                  reads=[R_bank[b], t["r_rz"]], writes=u["regs_out"])

        def run_units(units, tmps):
            prev = None
            for i, u in enumerate(units):
                attn_s1(u, tmps[i % 2])
                if prev is not None:
                    attn_s2(prev[0], prev[1])
                prev = (u, tmps[i % 2])
            if prev is not None:
                attn_s2(prev[0], prev[1])

        def dump(name, ap, regs, shape, dt):
            if name in debug:
                dbg_out[name] = nc.dram_tensor("dbg_" + name, list(shape), dt, kind="ExternalOutput").ap()
                final_ops.append(S.add("sp", i_dma(dbg_out[name], ap), reads=regs, dsem=S.dsem("dbg_" + name)))

        R_oa, R_ob = reg("oa"), reg("ob")
        R_mg = [reg(f"mg{n}") for n in range(3)]
        R_h2T = [reg(f"h2T{t}") for t in range(9)]
        R_act = reg("act")
        R_x1scr = [[reg(f"x1scr{p}_{t}") for t in range(8)] for p in range(NPASS)]
        D_cos, D_sin, D_mask = S.dsem("cos"), S.dsem("sin"), S.dsem("maskA")
        D_tabB = [S.dsem(f"tabB{k}") for k in range(5)]
        R_tabB = [reg(f"tabB{k}") for k in range(5)]
        D_tabBx = S.dsem("tabBx")
        D_xt2 = [S.dsem(f"xt2_{i}") for i in range(2)]
        D_x1st = [S.dsem(f"x1st_{i}") for i in range(2)]
        D_gb2 = S.dsem("gb2")
        D_oa = S.dsem("oa_w")
        D_x1s = [S.dsem(f"x1s{i}") for i in range(8)]
        D_yst = [S.dsem(f"yst{i}") for i in range(4)]
        D_xtA = [S.dsem(f"xtA{i}") for i in range(2)]
        D_gbA = S.dsem("gbA")

        for p in range(npass):
            xt = [carve(0, 4096, F32), carve(4096, 4096, F32)]
            xn = [carve(8192, 2048), carve(10240, 2048)]
            junk = carve(12288, 2048)
            gbt = carve(14336, 4096, F32)
            D_xt = D_xtA
            D_gb = D_gbA
            S.barrier()
            prev = None
            for c in range(NCH + 1):
                if c < NCH:
                    s = c % 2
                    S.add("sp", i_dma(xt[s][:, :], xe[p, c * 128:(c + 1) * 128, :]),
                          writes=[reg(f"xt{s}")], dsem=D_xt[s])
                    r_xt, r_xn = reg(f"xt{s}"), reg(f"xtn{s}")
                    r_ss, r_rs = reg(f"ss{s}"), reg(f"rs{s}")
                    S.add("act", i_act(junk[:, :], xt[s][:, :], AF.Square, accum_out=ssb[:, s:s + 1]),
                          reads=[r_xt], writes=[reg("junk"), r_ss])
                    S.add("act", i_act(rstdb[:, s:s + 1], ssb[:, s:s + 1], AF.Ln, scale=1.0 / D, bias=epsc[:, 0:1]),
                          reads=[r_ss, R_const], writes=[r_rs])
                    S.add("act", i_act(rstdb[:, s:s + 1], rstdb[:, s:s + 1], AF.Exp, scale=-0.5),
                          reads=[r_rs], writes=[r_rs])
                    S.add("act", i_act(xn[s][:, :], xt[s][:, :], AF.Copy, scale=rstdb[:, s:s + 1]),
                          reads=[r_xt, r_rs], writes=[r_xn])
                if prev is not None:
                    cc, ss_ = prev
                    r_xn = reg(f"xtn{ss_}")
                    for half in range(2):
                        b = next_bank()
                        for k in range(8):
                            kk = half * 8 + k
                            S.add("pe", i_tr(psb[:, b, k * 128:(k + 1) * 128], xn[ss_][:, kk * 128:(kk + 1) * 128], ident[:, :]),
                                  reads=[r_xn, R_const], writes=[R_bank[b]])
                        for k in range(8):
                            kk = half * 8 + k
                            S.add("dve", i_ts(hT3[:, kk, cc * 128:(cc + 1) * 128], psb[:, b, k * 128:(k + 1) * 128],
                                              gmixfm[:, kk:kk + 1], None, ALU.mult),
                                  reads=[R_bank[b], R_const], writes=[R_hT[cc]])
                prev = (c, c % 2) if c < NCH else None
            if "hT" in debug and p == 0:
                dbg_out["hT"] = nc.dram_tensor("dbg_hT", [128, 16 * NKV], BF16, kind="ExternalOutput").ap()
                final_ops.append(S.add("sp", i_dma(dbg_out["hT"], HT[:, :]), reads=R_hT, dsem=S.dsem("dbg_hT")))
            if stop_after == "A":
                break

            S.barrier()
            cosb = carve(0, 3328, F32)
            sinb = carve(3328, 3328, F32)
            maskA = carve(6656, 2304, F32).rearrange("p (a j q) -> p a j q", a=3, j=3)
            qA = carve(8960, 4104).rearrange("p (h t) -> p h t", h=4)
            kTa = carve(13064, 1664)
            vTa = carve(14728, 1664)
            Va = carve(16392, 1664).rearrange("p (c d) -> p c d", c=13)
            tmpA = [mk_tmp(dict(sq=(18056, 1024, F32), rstd=(19080, 1024, F32), y32=(20104, 1024, F32),
                                t1=(21128, 1024, F32), t2=(22152, 1024, F32)), "tA0"),
                    mk_tmp(dict(sq=(23176, 1024, F32), rstd=(24200, 1024, F32), y32=(25224, 1024, F32),
                                t1=(26248, 1024, F32), t2=(27272, 1024, F32)), "tA1")]
            atA = [mk_tmp(dict(P=(28296 + i * 640, 384, BF16), rz=(28296 + i * 640 + 384, 256, F32)), f"aA{i}")
                   for i in range(4)]
            maskbf = carve(30856, 1152).rearrange("p (a j q) -> p a j q", a=3, j=3)
            r_qA = [reg(f"qA{i}") for i in range(4)]
            r_kTa, r_vTa, r_Va = reg("kTa"), reg("vTa"), reg("Va")
            S.add("sp", i_dma(cosb[:, :], cos_d[p]), writes=[reg("rope")], dsem=D_cos)
            S.add("sp", i_dma(sinb[:, :], sin_d[p]), writes=[reg("rope")], dsem=D_sin, join=True)
            S.add("sp", i_dma(carve(6656, 2304, F32), maskA_d[p]), writes=[reg("maskA32")], dsem=D_mask)
            S.add("dve", i_vcopy(carve(30856, 1152), carve(6656, 2304, F32)), reads=[reg("maskA32")], writes=[reg("maskA")])
            tq = [0]

            def nxt_tmpA():
                tq[0] += 1
                return tmpA[tq[0] % 2]

            kv_list = [(n, KVN[n][1] - KVN[n][0]) for n in range(4)]
            q_list = [(0, 512), (1, 512), (2, 2)]
            def load_kv(g_):
                return wload2(16, 256, [(0, 16, 0, wsrc(w_in, 1024 + g_ * 128, 128)),
                                        (0, 16, 128, wsrc(w_in, 1280 + g_ * 128, 128))])

            ws_kv_next = load_kv(0)
            for g in range(2):
                ws = ws_kv_next
                ws2 = wload2(16, 512, [(0, 16, 0, wsrc(w_in, g * 512, 512))])
                for (n, nn) in kv_list:
                    lo, hi = KVN[n]
                    b = gemm_tile(ws, 16, 256, 0, lambda kc, lo=lo, hi=hi: hT3[:, kc, lo:hi], kv_regs(n), nn)
                    round_end(qknorm_gen(b, nn, gains[:, 1:2], kTa[:, lo:hi], [r_kTa], nxt_tmpA(),
                                         rope=(cosb[:, lo:hi], sinb[:, lo:hi])))
                for (n, nn) in kv_list:
                    lo, hi = KVN[n]
                    b = gemm_tile(ws, 16, 256, 128, lambda kc, lo=lo, hi=hi: hT3[:, kc, lo:hi], kv_regs(n), nn)
                    S.add("act", i_act(vTa[:, lo:hi], ps[:, b, 0:nn], AF.Copy), reads=[R_bank[b]], writes=[r_vTa])
                    held.discard(b)
                    round_end(None)
                pipe.push(vtr_gen(vTa, r_vTa, Va, r_Va))
                if g == 0:
                    ws_kv_next = load_kv(1)
                for hh in range(4):
                    for (n, nn) in q_list:
                        b = gemm_tile(ws2, 16, 512, hh * 128, lambda kc, n=n: hq(kc, n), hq_regs(n), nn)
                        round_end(qknorm_gen(b, nn, gains[:, 0:1], qA[:, hh, QO[n][0]:QO[n][1]], [r_qA[hh]], nxt_tmpA(),
                                             rope=(tcol(cosb, n), tcol(sinb, n))), nunits=4, tmps=atA)
                    hq_ = g * 4 + hh
                    batch = []
                    for i in range(8):
                        c = 3 + i
                        kind = 0 if i == 0 else (2 if i == 7 else 1)
                        batch.append(dict(q=qA[:, hh, i * 128:(i + 1) * 128], nq=128,
                                                  kch=[kTa[:, cc * 128:(cc + 1) * 128] for cc in (c - 1, c, c + 1)],
                                                  vch=[Va[:, cc, :] for cc in (c - 1, c, c + 1)],
                                                  tab=None, mask_pe=[maskbf[:, kind, j, :] for j in range(3)],
                                                  sink=esink[:, hq_:hq_ + 1],
                                                  out=out_aT[:, hq_, i * 128:(i + 1) * 128],
                                                  regs_qk=[r_qA[hh], r_kTa], regs_v=[r_Va], regs_tab=[reg("maskA")],
                                                  regs_out=[R_oa]))
                    for (col, chunks, qi) in ((1024, (1, 2, 3), 127), (1025, (10, 11, 12), 0)):
                        batch.append(dict(q=qA[:, hh, col:col + 1], nq=1,
                                                  kch=[kTa[:, cc * 128:(cc + 1) * 128] for cc in chunks],
                                                  vch=[Va[:, cc, :] for cc in chunks],
                                                  tab=None, mask_pe=[maskbf[:, 1, j, qi:qi + 1] for j in range(3)],
                                                  sink=esink[:, hq_:hq_ + 1],
                                                  out=out_aT[:, hq_, col:col + 1],
                                                  regs_qk=[r_qA[hh], r_kTa], regs_v=[r_Va], regs_tab=[reg("maskA")],
                                                  regs_out=[R_oa]))
                    defer_units(2, batch)
                drain_units(2, atA)
                if g == 0 and p == 0:
                    dump("kTa", kTa[:, :], [r_kTa], [128, 1664], BF16)
                    dump("qA", carve(8960, 4104), r_qA, [128, 4104], BF16)
                    dump("Va", carve(16392, 1664), [r_Va], [128, 1664], BF16)
            if p == 0:
                dump("oa", BIG[:, OA0:OA0 + 8208], [R_oa], [128, 8208], BF16)
            if stop_after == "B":
                break

            S.barrier()
            tabB = carve(0, 6912, F32).rearrange("p (j q) -> p j q", j=27)
            qTb = [carve(6912, 1026), carve(7938, 1026)]
            kTb = [carve(8964, 1664), carve(10628, 1664)]
            vTb = carve(12292, 1664)
            Vb = [carve(13956, 1664).rearrange("p (c d) -> p c d", c=13),
                  carve(15620, 1664).rearrange("p (c d) -> p c d", c=13)]
            tmpB = [mk_tmp(dict(sq=(17284, 1024, F32), rstd=(18308, 1024, F32)), "tB0"),
                    mk_tmp(dict(sq=(19332, 1024, F32), rstd=(20356, 1024, F32)), "tB1")]
            atB = [mk_tmp(dict(E=(21380 + i * 2560, 1536, F32), P=(21380 + i * 2560 + 1536, 768, BF16),
                               rz=(21380 + i * 2560 + 2304, 256, F32)), f"aB{i}") for i in range(4)]
            S.add("sp", i_dma(tabBx[:, :], tabBx_d[p]), writes=[reg("tabBx")], dsem=D_tabBx)
            tqb = [0]

            def nxt_tmpB():
                tqb[0] += 1
                return tmpB[tqb[0] % 2]

            for h in range(8):
                s_ = h % 2
                r_q, r_k, r_vT, r_V = reg(f"qTb{s_}"), reg(f"kTb{s_}"), reg("vTb"), reg(f"Vb{s_}")
                ws = wload2(16, 384, [(0, 16, 0, wsrc(w_in, 1536 + h * 128, 128)),
                                      (0, 16, 128, wsrc(w_in, 2560 + h * 128, 128)),
                                      (0, 16, 256, wsrc(w_in, 3584 + h * 128, 128))])
                for (n, nn) in kv_list:
                    lo, hi = KVN[n]
                    b = gemm_tile(ws, 16, 384, 128, lambda kc, lo=lo, hi=hi: hT3[:, kc, lo:hi], kv_regs(n), nn)
                    round_end(qknorm_gen(b, nn, gains[:, 3:4], kTb[s_][:, lo:hi], [r_k], nxt_tmpB()), nunits=2, tmps=atB)
                for (n, nn) in kv_list:
                    lo, hi = KVN[n]
                    b = gemm_tile(ws, 16, 384, 256, lambda kc, lo=lo, hi=hi: hT3[:, kc, lo:hi], kv_regs(n), nn)
                    S.add("act", i_act(vTb[:, lo:hi], ps[:, b, 0:nn], AF.Copy), reads=[R_bank[b]], writes=[r_vT])
                    held.discard(b)
                    round_end(None, nunits=2, tmps=atB)
                pipe.push(vtr_gen(vTb, r_vT, Vb[s_], r_V))
                for (n, nn) in q_list:
                    b = gemm_tile(ws, 16, 384, 0, lambda kc, n=n: hq(kc, n), hq_regs(n), nn)
                    round_end(qknorm_gen(b, nn, gains[:, 2:3], qTb[s_][:, QO[n][0]:QO[n][1]], [r_q], nxt_tmpB()),
                              nunits=2, tmps=atB)
                assert not pending_units and not deferred, "units of the previous head must be registered by now"

                def load_tabs(h=h):
                    for kind in range(5):
                        off = B_KOFF[kind]
                        nchk = len(B_KINDS[kind][1])
                        S.add("sp", i_dma(tabB[:, off:off + nchk, :],
                                          tabB_d[p, h, :, off * 128:(off + nchk) * 128].rearrange("p (j q) -> p j q", q=128)),
                              writes=[R_tabB[kind]], dsem=D_tabB[kind])
                batch = [("call", load_tabs)]
                for i in range(8):
                    kind = 0 if i == 0 else 1 if i == 1 else 3 if i == 6 else 4 if i == 7 else 2
                    chunks = B_KINDS[kind][1] if kind != 2 else list(range(i + 1, i + 6))
                    off = B_KOFF[kind]
                    batch.append(dict(q=qTb[s_][:, i * 128:(i + 1) * 128], nq=128,
                                              kch=[kTb[s_][:, cc * 128:(cc + 1) * 128] for cc in chunks],
                                              vch=[Vb[s_][:, cc, :] for cc in chunks],
                                              tab=tabB[:, off:off + len(chunks), :], sink=None,
                                              out=out_bT[:, h, i * 128:(i + 1) * 128],
                                              regs_qk=[r_q, r_k], regs_v=[r_V], regs_tab=[R_tabB[kind]], regs_out=[R_ob]))
                for (col, chunks, toff) in ((1024, list(range(0, 5)), 0), (1025, list(range(9, 13)), 5)):
                    batch.append(dict(q=qTb[s_][:, col:col + 1], nq=1,
                                              kch=[kTb[s_][:, cc * 128:(cc + 1) * 128] for cc in chunks],
                                              vch=[Vb[s_][:, cc, :] for cc in chunks],
                                              tab=tabBx[:, h * 9 + toff:h * 9 + toff + len(chunks)].rearrange("p (j q) -> p j q", q=1),
                                              sink=None, out=out_bT[:, h, col:col + 1],
                                              regs_qk=[r_q, r_k], regs_v=[r_V], regs_tab=[reg("tabBx")], regs_out=[R_ob]))
                defer_units(1, batch)
            drain_units(2, atB)
            if p == 0:
                dump("ob", BIG[:, OB0:OB0 + 8208], [R_ob], [128, 8208], BF16)
            if stop_after == "C":
                break

            S.barrier()
            mg = carve(0, 16416).rearrange("p (k t) -> p k t", k=16)
            sga = [carve(16416, 1024, F32), carve(17440, 1024, F32)]
            sgb = [carve(18464, 1024, F32), carve(19488, 1024, F32)]
            m1 = [carve(20512, 1024, F32), carve(21536, 1024, F32)]
            m2 = [carve(22560, 1024, F32), carve(23584, 1024, F32)]
            for j in range(16):
                ws = wload2(48, 128, [(0, 16, 0, wsrc(w_in, 4608 + j * 128, 128)),
                                      (16, 32, 0, wsrc(w_in, 6656 + j * 128, 128)),
                                      (32, 40, 0, wsrc(w_ba, j * 128, 128)),
                                      (40, 48, 0, wsrc(w_bb, j * 128, 128))])
                w3 = wv(ws, 48, 128)
                for n in range(3):
                    nn = QN[n]
                    o0, o1 = QO[n]
                    t = (j * 3 + n) % 2
                    r_sga, r_sgb, r_m1, r_m2 = reg(f"sga{t}"), reg(f"sgb{t}"), reg(f"m1{t}"), reg(f"m2{t}")
                    b1 = next_bank()
                    for kc in range(16):
                        S.add("pe", i_mm(ps[:, b1, 0:nn], w3[:, kc, :], hq(kc, n), kc == 0, kc == 15),
                              reads=[R_w[ws]] + hq_regs(n), writes=[R_bank[b1]])
                    S.add("act", i_act(sga[t][:, 0:nn], ps[:, b1, 0:nn], AF.Sigmoid), reads=[R_bank[b1]], writes=[r_sga])
                    b2 = next_bank()
                    for kc in range(16):
                        S.add("pe", i_mm(ps[:, b2, 0:nn], w3[:, 16 + kc, :], hq(kc, n), kc == 0, kc == 15),
                              reads=[R_w[ws]] + hq_regs(n), writes=[R_bank[b2]])
                    S.add("act", i_act(sgb[t][:, 0:nn], ps[:, b2, 0:nn], AF.Sigmoid), reads=[R_bank[b2]], writes=[r_sgb])
                    b3 = next_bank()
                    for kc in range(8):
                        S.add("pe", i_mm(ps[:, b3, 0:nn], w3[:, 32 + kc, :], out_aT[:, kc, o0:o1], kc == 0, kc == 7),
                              reads=[R_w[ws], R_oa], writes=[R_bank[b3]])
                    S.add("dve", i_tt(m1[t][:, 0:nn], ps[:, b3, 0:nn], sga[t][:, 0:nn], ALU.mult),
                          reads=[R_bank[b3], r_sga], writes=[r_m1])
                    b4 = next_bank()
                    for kc in range(8):
                        S.add("pe", i_mm(ps[:, b4, 0:nn], w3[:, 40 + kc, :], out_bT[:, kc, o0:o1], kc == 0, kc == 7),
                              reads=[R_w[ws], R_ob], writes=[R_bank[b4]])
                    S.add("dve", i_tt(m2[t][:, 0:nn], ps[:, b4, 0:nn], sgb[t][:, 0:nn], ALU.mult),
                          reads=[R_bank[b4], r_sgb], writes=[r_m2])
                    S.add("dve", i_tt(mg[:, j, o0:o1], m1[t][:, 0:nn], m2[t][:, 0:nn], ALU.add),
                          reads=[r_m1, r_m2], writes=[R_mg[n]])
            if p == 0:
                dump("mg", carve(0, 16416), R_mg, [128, 16416], BF16)
            if stop_after == "D":
                break

            S.barrier()
            xt2 = [carve(16416, 4096, F32), carve(20512, 4096, F32)]
            x1t = [carve(24608, 4096, F32), carve(28704, 4096, F32)]
            xn2 = [BIG[:, OB0:OB0 + 2048], BIG[:, OB0 + 2048:OB0 + 4096]]
            gb2 = BIG[:, OB0 + 4096:OB0 + 8192].bitcast(F32)
            h2T = HT[:, 0:16416].rearrange("p (k t) -> p k t", k=16)
            S.add("sp", i_dma(gb2[:, :], gffn_d), writes=[reg("gb2")], dsem=D_gb2)
            wo = []
            for fg in range(3):
                ws = wload2(16, 512, [(0, 16, 0, wsrc(w_out, fg * 512, 512))])
                wo.append((wv(ws, 16, 512), R_w[ws]))
            w4 = BIG[:, OA0:OA0 + 8192].rearrange("p (k m) -> p k m", k=16)
            S.add("pool", i_dma(w4, wsrc(w_out, 1536, 512)), writes=[R_oa], dsem=D_oa)
            wo.append((w4, R_oa))
            prev = None
            for tt in range(10):
                if tt < 9:
                    s_ = tt % 2
                    rows = 128 if tt < 8 else 2
                    cols = (tt * 128, tt * 128 + 128) if tt < 8 else (1024, 1026)
                    r_xt2, r_x1t = reg(f"xt2_{s_}"), reg(f"x1t{s_}")
                    if tt < 8:
                        S.add("sp", i_dma(xt2[s_][:, :], xe[p, HL + tt * 128:HL + (tt + 1) * 128, :]),
                              writes=[r_xt2], dsem=D_xt2[s_])
                    else:
                        S.add("sp", i_dma(xt2[s_][0:1, :], xe[p, HL - 1:HL, :]), writes=[r_xt2], dsem=D_xt2[s_])
                        S.add("sp", i_dma(xt2[s_][1:2, :], xe[p, HL + T:HL + T + 1, :]), writes=[r_xt2],
                              dsem=D_xt2[s_], join=True)
                    for fg in range(4):
                        b = next_bank()
                        w3, rw = wo[fg]
                        for kc in range(16):
                            S.add("pe", i_mm(ps[0:rows, b, 0:512], mg[:, kc, cols[0]:cols[1]], w3[:, kc, :], kc == 0, kc == 15),
                                  reads=[rw, R_mg[min(tt // 4, 2)]], writes=[R_bank[b]])
                        S.add("dve", i_tt(x1t[s_][0:rows, fg * 512:(fg + 1) * 512], ps[0:rows, b, 0:512],
                                          xt2[s_][0:rows, fg * 512:(fg + 1) * 512], ALU.add),
                              reads=[R_bank[b], r_xt2], writes=[r_x1t])
                    if tt < 8:
                        S.add("sp", i_dma(x1scr[p * T + tt * 128:p * T + (tt + 1) * 128, :], x1t[s_][:, :]),
                              reads=[r_x1t], writes=[R_x1scr[p][tt]], dsem=D_x1st[s_])
                    norm_stage1(s_, x1t, xn2, None, gb2, rows, pfx="x1t", gbreg="gb2")
                if prev is not None:
                    ptt, ps_, prow, pcols = prev
                    norm_stage2(ps_, xn2, prow,
                                lambda half, pcols=pcols: h2T[:, half * 8:(half + 1) * 8, pcols[0]:pcols[1]],
                                [R_h2T[ptt]], ptt, pfx="x1t")
                prev = (tt, tt % 2, 128 if tt < 8 else 2, (tt * 128, tt * 128 + 128) if tt < 8 else (1024, 1026)) if tt < 9 else None
            if p == 0:
                dump("h2T", HT[:, 0:16416], R_h2T, [128, 16416], BF16)
            if stop_after == "E":
                break

            S.barrier()
            act_t = BIG[:, 0:45056].rearrange("p (j t) -> p j t", j=44)
            ubg = HT[:, 16416:18468].bitcast(F32)
            ubv = HT[:, 18468:20520].bitcast(F32)
            tg = HT[:, 20520:22568].bitcast(F32)
            tv = HT[:, 22568:24616].bitcast(F32)
            cw3 = cw[:, :].rearrange("p (f k) -> p f k", k=3)
            r_ubg, r_ubv, r_tg, r_tv = reg("ubg"), reg("ubv"), reg("tg"), reg("tv")

            def h2_regs(n):
                return R_h2T[0:4] if n == 0 else R_h2T[4:8] if n == 1 else [R_h2T[8]]

            for u in range(22):
                ws = wload2(16, 512, [(0, 16, 0, wsrc(w_up, 2 * u * 128, 256)),
                                      (0, 16, 256, wsrc(w_up, DFF + 2 * u * 128, 256))])
                for q in range(2):
                    j = 2 * u + q
                    for (coff, ub, r_ub, tt_, r_t, f) in ((q * 128, ubg, r_ubg, tg, r_tg, j),
                                                           (256 + q * 128, ubv, r_ubv, tv, r_tv, NFF + j)):
                        def cons_u(n, b, ub=ub, r_ub=r_ub):
                            if n == 2:
                                S.add("dve", i_tt(ub[:, 0:1026:1025], ps[:, b, 0:2], uflag[:, 2 * p:2 * p + 2], ALU.mult),
                                      reads=[R_bank[b], R_const], writes=[r_ub])
                                return
                            dst = ub[:, 1:513] if n == 0 else ub[:, 513:1025]
                            S.add("act", i_act(dst, ps[:, b, 0:QN[n]], AF.Copy), reads=[R_bank[b]], writes=[r_ub])
                        gemm_fm(ws, 16, 512, coff, 0, lambda kc, n: h2T[:, kc, QO[n][0]:QO[n][1]], h2_regs,
                                [(0, 512), (1, 512), (2, 2)], cons_u)
                        S.add("dve", i_ts(tt_[:, :], ub[:, 1:1025], cw3[:, f, 1:2], cb[:, f:f + 1], ALU.mult, ALU.add),
                              reads=[r_ub, R_const], writes=[r_t])
                        S.add("dve", i_stt(tt_[:, :], ub[:, 0:1024], cw3[:, f, 0:1], tt_[:, :], ALU.mult, ALU.add),
                              reads=[r_ub, r_t, R_const], writes=[r_t])
                        S.add("dve", i_stt(tt_[:, :], ub[:, 2:1026], cw3[:, f, 2:3], tt_[:, :], ALU.mult, ALU.add),
                              reads=[r_ub, r_t, R_const], writes=[r_t])
                    S.add("act", i_act(tg[:, :], tg[:, :], AF.Silu), reads=[r_tg], writes=[r_tg])
                    S.add("dve", i_tt(act_t[:, j, :], tg[:, :], tv[:, :], ALU.mult), reads=[r_tg, r_tv], writes=[R_act])
            if p == 0:
                dump("act", BIG[:, 0:45056], [R_act], [128, 45056], BF16)
            if stop_after == "F":
                break

            S.barrier()
            x1s = [HT[:, i * 1024:(i + 1) * 1024].bitcast(F32) for i in range(8)]
            yst = [HT[:, 8192 + i * 1024:8192 + (i + 1) * 1024].bitcast(F32) for i in range(4)]
            for fg in range(4):
                for tt in range(8):
                    S.add("sp", i_dma(x1s[tt][:, :], x1scr[p * T + tt * 128:p * T + (tt + 1) * 128, fg * 512:(fg + 1) * 512]),
                          reads=[R_x1scr[p][tt]], writes=[reg(f"x1s{tt}")], dsem=D_x1s[tt])
                for ku in range(4):
                    ws = wload2(11, 512, [(0, 11, 0, w_down[ku * 1408:(ku + 1) * 1408, fg * 512:(fg + 1) * 512]
                                           .rearrange("(k p) m -> p k m", p=128))])
                    w3 = wv(ws, 11, 512)
                    for tt in range(8):
                        for kc in range(11):
                            S.add("pe", i_mm(ps[:, tt, :], act_t[:, ku * 11 + kc, tt * 128:(tt + 1) * 128], w3[:, kc, :],
                                             ku == 0 and kc == 0, ku == 3 and kc == 10),
                                  reads=[R_w[ws], R_act], writes=[R_bank[tt]])
                for tt in range(8):
                    r = (fg * 8 + tt) % 4
                    S.add("dve", i_tt(yst[r][:, :], ps[:, tt, :], x1s[tt][:, :], ALU.add),
                          reads=[R_bank[tt], reg(f"x1s{tt}")], writes=[reg(f"yst{r}")])
                    final_ops.append(S.add("sp", i_dma(out_d[p * T + tt * 128:p * T + (tt + 1) * 128, fg * 512:(fg + 1) * 512],
                                                       yst[r][:, :]),
                                           reads=[reg(f"yst{r}")], dsem=D_yst[r]))

        S.barrier()
        S.final_wait("sp", final_ops)
        S.emit()
    return nc, dbg_out


def _rope_tables(t0):
    pos = (t0 - HL + np.arange(NKV)).astype(np.float32)
    inv_freq = (np.float32(10000.0) ** (-np.arange(64, dtype=np.float32) * np.float32(2.0 / 128))).astype(np.float32)
    ang = pos[:, None] * inv_freq[None, :]
    cos = np.cos(ang).astype(np.float32)
    sin = np.sin(ang).astype(np.float32)
    cosT = np.concatenate([cos, cos], axis=1).T.copy()
    sinT = np.concatenate([sin, -sin], axis=1).T.copy()
    return cosT, sinT


def _maskA(t0):
    out = np.full((128, 3, 3, 128), NEG, np.float32)
    k = np.arange(128)[:, None]
    q = np.arange(128)[None, :]
    for kind, i in enumerate((0, 3, 7)):
        c = 3 + i
        for j in range(3):
            kpos = t0 - HL + (c - 1 + j) * 128 + k
            qpos = t0 + i * 128 + q
            valid = (np.abs(qpos - kpos) <= 128) & (kpos >= 0) & (kpos < S_LEN)
            out[:, kind, j, :] = np.where(valid, 0.0, NEG / SCALE)
    return out.reshape(128, 9 * 128)


def _b_tab(rpb_h, qr, qc, kr, kc):
    qr_, qc_ = qr[None, :], qc[None, :]
    kr_, kc_ = kr[:, None], kc[:, None]
    rs = np.clip(qr_ - 4, 0, 64 - 8)
    cs = np.clip(qc_ - 8, 0, 64 - 16)
    valid = (kr_ >= rs) & (kr_ < rs + 8) & (kc_ >= cs) & (kc_ < cs + 16) & (kr_ >= 0) & (kr_ < 64)
    ir = np.clip(kr_ - qr_ + 7, 0, 14)
    ic = np.clip(kc_ - qc_ + 15, 0, 30)
    return np.where(valid, rpb_h[ir, ic], np.float32(NEG)).astype(np.float32)


B_KINDS = [(0, list(range(1, 7))), (1, list(range(2, 7))), (2, list(range(3, 8))),
           (6, list(range(7, 12))), (7, list(range(7, 13)))]
B_KOFF = [0, 6, 11, 16, 21]


def _tabB(rpb, t0):
    r0 = t0 // 64
    tab = np.empty((8, 128, 27, 128), np.float32)
    kk = np.arange(128)
    for h in range(8):
        for (i, chunks), off in zip(B_KINDS, B_KOFF):
            qr = r0 + 2 * i + kk // 64
            qc = kk % 64
            for j, cc in enumerate(chunks):
                kr = r0 - 6 + 2 * cc + kk // 64
                kc = kk % 64
                tab[h, :, off + j, :] = _b_tab(rpb[h], qr, qc, kr, kc)
    tabx = np.empty((128, 8, 9), np.float32)
    for h in range(8):
        for j, cc in enumerate(range(0, 5)):
            kr = r0 - 6 + 2 * cc + kk // 64
            tabx[:, h, j] = _b_tab(rpb[h], np.array([r0 - 1]), np.array([63]), kr, kk % 64)[:, 0]
        for j, cc in enumerate(range(9, 13)):
            kr = r0 - 6 + 2 * cc + kk // 64
            tabx[:, h, 5 + j] = _b_tab(rpb[h], np.array([r0 + 16]), np.array([0]), kr, kk % 64)[:, 0]
    return tab.reshape(8, 128, 27 * 128), tabx.reshape(128, 72)


def host_prepare(inputs, cores=range(8)):
    f = lambda a: np.ascontiguousarray(np.asarray(a, dtype=np.float32))
    x = f(inputs["x"])
    shared = {
        "w_in": f(inputs["w_in"][0]), "w_ba": f(inputs["w_branch_a"][0]), "w_bb": f(inputs["w_branch_b"][0]),
        "w_out": f(inputs["w_out"][0]), "w_up": f(inputs["w_up"][0]), "w_down": f(inputs["w_down"][0]),
        "gmix_b": np.ascontiguousarray(np.broadcast_to(f(inputs["norm_mix"][0])[None, :], (128, D))),
        "gffn_b": np.ascontiguousarray(np.broadcast_to(f(inputs["norm_ffn"][0])[None, :], (128, D))),
        "gmix_fm": np.ascontiguousarray(f(inputs["norm_mix"][0]).reshape(16, 128).T),
        "gains": np.ascontiguousarray(np.stack([f(inputs["a_q_norm"][0]), f(inputs["a_k_norm"][0]),
                                                f(inputs["b_q_norm"][0]), f(inputs["b_k_norm"][0])], axis=1)),
        "sink_b": np.ascontiguousarray(np.broadcast_to(f(inputs["a_sink"][0])[None, :], (128, 8))),
        "cw_fm": np.ascontiguousarray(f(inputs["conv_w"][0]).T.reshape(88, 128, 3).transpose(1, 0, 2).reshape(128, 264)),
        "cb_fm": np.ascontiguousarray(f(inputs["conv_b"][0]).reshape(88, 128).T),
        "ident": np.eye(128, dtype=np.float32),
    }
    rm = np.zeros((128, 128), np.float32)
    for m in range(64):
        rm[m + 64, m] = -1.0
        rm[m, m + 64] = 1.0
    shared["rm"] = rm
    rpb = f(inputs["b_rpb"][0])
    per_t0 = {}
    for t0 in range(0, S_LEN, T):
        cosT, sinT = _rope_tables(t0)
        tb, tbx = _tabB(rpb, t0)
        per_t0[t0] = (cosT, sinT, _maskA(t0), tb, tbx)
    in_maps = []
    for c in cores:
        b, half = c // 2, c % 2
        m = dict(shared)
        xes, cs, sn, ma, tb, tbx = [], [], [], [], [], []
        for p in range(NPASS):
            t0 = half * TOK_CORE + p * T
            lo, hi = t0 - HL, t0 - HL + NKV
            xp = np.zeros((NKV, D), np.float32)
            a, bb = max(lo, 0), min(hi, S_LEN)
            xp[a - lo:bb - lo] = x[b, a:bb]
            xes.append(xp)
            t = per_t0[t0]
            cs.append(t[0]); sn.append(t[1]); ma.append(t[2]); tb.append(t[3]); tbx.append(t[4])
        m["xe"] = np.stack(xes)
        m["cos_t"] = np.stack(cs)
        m["sin_t"] = np.stack(sn)
        m["maskA"] = np.stack(ma)
        m["tabB"] = np.stack(tb)
        m["tabBx"] = np.stack(tbx)
        fl = np.zeros((128, 2 * NPASS), np.float32)
        for p in range(NPASS):
            t0 = half * TOK_CORE + p * T
            fl[:, 2 * p] = 1.0 if t0 - 1 >= 0 else 0.0
            fl[:, 2 * p + 1] = 1.0 if t0 + T < S_LEN else 0.0
        m["uflag"] = fl
        in_maps.append(m)
    return in_maps


_CACHE = {}


def kernel(**inputs):
    in_maps = host_prepare(inputs)
    if "nc" not in _CACHE:
        _CACHE["nc"] = build_program()[0]
    nc = _CACHE["nc"]
    res = run_bass_kernel_spmd(nc, in_maps, core_ids=list(range(8)))
    out = np.empty((4, S_LEN, D), np.float32)
    for c in range(8):
        b, half = c // 2, c % 2
        out[b, half * TOK_CORE:(half + 1) * TOK_CORE] = res.results[c]["out"]
    return out
```

```python
import contextlib
import numpy as np
import concourse.bass as bass
import concourse.mybir as mybir
from concourse.bass_utils import run_bass_kernel_spmd

F32 = mybir.dt.float32
BF16 = mybir.dt.bfloat16
ALU = mybir.AluOpType
AF = mybir.ActivationFunctionType

D = 2048
S_LEN = 4096
T = 1024
NCH = 13
NKV = NCH * 128
HL = 384
NQ = T + 2
DFF = 5632
NFF = DFF // 128
INC = 8704
EPS = 1e-6
NEG = -30000.0
SCALE = 1.0 / float(np.sqrt(128.0))
NPASS = 2
TOK_CORE = NPASS * T

ENGS = ("pe", "act", "dve", "pool", "sp")
CAP = 30000


class Region:
    __slots__ = ("name", "writers", "readers")

    def __init__(self, name):
        self.name = name
        self.writers = {}
        self.readers = {}


class DSem:
    __slots__ = ("name", "count", "sem")

    def __init__(self, name):
        self.name = name
        self.count = 0
        self.sem = None


class Op:
    __slots__ = ("eng", "fn", "deps", "dsem", "dval", "sigidx", "key", "signal")

    def __init__(self, eng, fn, dsem):
        self.eng = eng
        self.fn = fn
        self.deps = set()
        self.dsem = dsem
        self.dval = 0
        self.sigidx = 0
        self.signal = False
        self.key = eng if dsem is None else ("dma", id(self))


class Sched:
    def __init__(self, nc):
        self.nc = nc
        self.ops = {e: [] for e in ENGS}
        self.pending = {e: set() for e in ENGS}
        self.dsems = []
        self.dma_live = []
        self.last = {}
        self.final = {e: set() for e in ENGS}

    def dsem(self, name):
        d = DSem(name)
        self.dsems.append(d)
        return d

    def add(self, eng, fn, reads=(), writes=(), dsem=None, join=False):
        op = Op(eng, fn, dsem)
        deps = set(self.pending[eng])
        self.pending[eng] = set()
        is_dma = dsem is not None
        for r in reads:
            for k, w in r.writers.items():
                if eng == "pe" and k == "pe":
                    continue
                deps.add(w)
        if not join:
            for r in writes:
                for k, w in r.writers.items():
                    if eng == "pe" and k == "pe":
                        continue
                    deps.add(w)
                for k, w in r.readers.items():
                    if eng == "pe" and k == "pe":
                        continue
                    deps.add(w)
        op.deps = deps
        for r in writes:
            if join:
                r.writers[op.key] = op
            else:
                r.writers = {op.key: op}
                r.readers = {}
        for r in reads:
            if r in writes:
                continue
            r.readers[op.key] = op
        if is_dma:
            dsem.count += 16
            op.dval = dsem.count
            self.dma_live.append(op)
        else:
            self.last[eng] = op
        self.ops[eng].append(op)
        return op

    def barrier(self, engines=("pe", "act", "dve", "sp")):
        deps = set()
        for e in engines:
            if e in self.last:
                deps.add(self.last[e])
        keep = []
        for op in self.dma_live:
            if op.eng in engines:
                deps.add(op)
            else:
                keep.append(op)
        self.dma_live = keep
        for e in engines:
            self.pending[e] |= {d for d in deps if d.key != e}

    def final_wait(self, eng, ops):
        self.final[eng] |= set(ops)

    def emit(self):
        nc = self.nc
        needed = set()
        for e in ENGS:
            for op in self.ops[e]:
                needed |= op.deps
            needed |= self.final[e]
        counts = {}
        for e in ENGS:
            c = 0
            for op in self.ops[e]:
                if op.dsem is None and op in needed:
                    c += 1
                    op.sigidx = c
                    op.signal = True
            counts[e] = c
        with contextlib.ExitStack() as st:
            esems = {}
            for e in ENGS:
                n = (counts[e] + CAP - 1) // CAP
                esems[e] = [st.enter_context(nc.semaphore(f"c_{e}_{i}")) for i in range(n)]
            for d in self.dsems:
                if d.count > 0:
                    d.sem = st.enter_context(nc.semaphore(f"d_{d.name}"))

            def resolve(op):
                if op.dsem is not None:
                    return op.dsem.sem, op.dval
                i = (op.sigidx - 1) // CAP
                return esems[op.eng][i], op.sigidx - i * CAP

            block = st.enter_context(nc.Block())
            sched = self

            def make_body(e):
                def body(eng):
                    waited = {}

                    def do_wait(d):
                        sem, val = resolve(d)
                        k = sem.num
                        if waited.get(k, 0) < val:
                            eng.wait_ge(sem, val)
                            waited[k] = val

                    def wait_all(deps):
                        best = {}
                        for d in deps:
                            sem, val = resolve(d)
                            if val > best.get(sem.num, (None, 0))[1]:
                                best[sem.num] = (sem, val)
                        for k in sorted(best):
                            sem, val = best[k]
                            if waited.get(k, 0) < val:
                                eng.wait_ge(sem, val)
                                waited[k] = val

                    for op in sched.ops[e]:
                        wait_all(op.deps)
                        ins = op.fn(eng)
                        if op.dsem is not None:
                            ins.then_inc(op.dsem.sem, 16)
                        elif op.signal:
                            sem, _ = resolve(op)
                            ins.then_inc(sem, 1)
                    wait_all(sched.final[e])
                return body

            hw = {"pe": block.tensor, "act": block.scalar, "dve": block.vector,
                  "pool": block.gpsimd, "sp": block.sync}
            for e in ENGS:
                if self.ops[e] or self.final[e]:
                    hw[e](make_body(e))
        return counts


def i_mm(out, lhsT, rhs, start, stop):
    return lambda e: e.matmul(out, lhsT=lhsT, rhs=rhs, start=start, stop=stop)


def i_tr(out, in_, ident):
    return lambda e: e.transpose(out=out, in_=in_, identity=ident)


def i_act(out, in_, func, scale=None, accum_out=None, bias=None):
    kw = {}
    if bias is not None:
        kw["bias"] = bias
    if scale is not None:
        kw["scale"] = scale
    if accum_out is not None:
        kw["accum_out"] = accum_out
    return lambda e: e.activation(out=out, in_=in_, func=func, **kw)


def i_ts(out, in0, s1, s2, op0, op1=None):
    if op1 is None:
        return lambda e: e.tensor_scalar(out=out, in0=in0, scalar1=s1, scalar2=None, op0=op0)
    return lambda e: e.tensor_scalar(out=out, in0=in0, scalar1=s1, scalar2=s2, op0=op0, op1=op1)


def i_stt(out, in0, scalar, in1, op0, op1):
    return lambda e: e.scalar_tensor_tensor(out=out, in0=in0, scalar=scalar, in1=in1, op0=op0, op1=op1)


def i_tt(out, in0, in1, op):
    return lambda e: e.tensor_tensor(out=out, in0=in0, in1=in1, op=op)


def i_vcopy(out, in_):
    return lambda e: e.tensor_copy(out=out, in_=in_)


def i_recip(out, in_):
    return lambda e: e.reciprocal(out=out, in_=in_)


def i_dma(out, in_):
    return lambda e: e.dma_start(out=out, in_=in_)


def build_program(npass=NPASS, stop_after=None, debug=()):
    nc = bass.Bass("TRN2", target_bir_lowering=False)
    S = Sched(nc)

    def din(name, shape, dt=F32):
        return nc.dram_tensor(name, list(shape), dt, kind="ExternalInput").ap()

    xe = din("xe", [NPASS, NKV, D])
    w_in = din("w_in", [D, INC])
    w_ba = din("w_ba", [1024, D])
    w_bb = din("w_bb", [1024, D])
    w_out = din("w_out", [D, D])
    w_up = din("w_up", [D, 2 * DFF])
    w_down = din("w_down", [DFF, D])
    gmix_d = din("gmix_b", [128, D])
    gffn_d = din("gffn_b", [128, D])
    gains_d = din("gains", [128, 4])
    sink_d = din("sink_b", [128, 8])
    cw_d = din("cw_fm", [128, 88 * 3])
    cb_d = din("cb_fm", [128, 88])
    ident_d = din("ident", [128, 128])
    rm_d = din("rm", [128, 128])
    cos_d = din("cos_t", [NPASS, 128, NKV])
    sin_d = din("sin_t", [NPASS, 128, NKV])
    maskA_d = din("maskA", [NPASS, 128, 9 * 128])
    tabB_d = din("tabB", [NPASS, 8, 128, 27 * 128])
    tabBx_d = din("tabBx", [NPASS, 128, 72])
    uflag_d = din("uflag", [128, 2 * NPASS])
    out_d = nc.dram_tensor("out", [TOK_CORE, D], F32, kind="ExternalOutput").ap()
    x1scr = nc.dram_tensor("x1scr", [TOK_CORE, D], F32).ap()
    dbg_out = {}

    st = contextlib.ExitStack()
    with st:
        def sb(name, shape, dt):
            return st.enter_context(nc.sbuf_tensor("sb_" + name, list(shape), dt))

        ident = sb("ident", [128, 128], BF16)
        ident_f = sb("ident_f", [128, 128], F32)
        ones_f = sb("ones_f", [128, 128], F32)
        ones_b = sb("ones_b", [128, 128], BF16)
        ones_ms = sb("ones_ms", [128, 128], BF16)
        rm_f = sb("rm_f", [128, 128], F32)
        rm_b = sb("rm_b", [128, 128], BF16)
        gains = sb("gains", [128, 4], F32)
        esink = sb("esink", [128, 8], F32)
        cw = sb("cw", [128, 88 * 3], F32)
        cb = sb("cb", [128, 88], F32)
        tabBx = sb("tabBx", [128, 72], F32)
        ssb = sb("ssb", [128, 4], F32)
        rstdb = sb("rstdb", [128, 4], F32)
        epsc = sb("epsc", [128, 1], F32)
        uflag = sb("uflag", [128, 2 * NPASS], F32)
        HT = sb("HT", [128, 16 * NKV], BF16)
        WR = [sb(f"WR{i}", [128, 8192], BF16) for i in range(3)]
        NBIG = 51000
        BIG = sb("BIG", [128, NBIG], BF16)
        ps = st.enter_context(nc.psum_tensor("ps", [128, 8, 512], F32))
        psb = ps.bitcast(BF16)

        OA0 = NBIG - 2 * 8208
        OB0 = NBIG - 8208

        def carve(off, n, dt=BF16):
            v = BIG[:, off:off + n]
            return v.bitcast(F32) if dt == F32 else v

        hT3 = HT[:, :].rearrange("p (k t) -> p k t", k=16)
        out_aT = BIG[:, OA0:OA0 + 8208].rearrange("p (h t) -> p h t", h=8)
        out_bT = BIG[:, OB0:OB0 + 8208].rearrange("p (h t) -> p h t", h=8)

        R = {}

        def reg(name):
            if name not in R:
                R[name] = Region(name)
            return R[name]

        R_bank = [reg(f"bank{b}") for b in range(8)]
        R_w = [reg(f"w{i}") for i in range(3)]
        D_w = [S.dsem(f"w{i}") for i in range(3)]
        R_hT = [reg(f"hT{c}") for c in range(NCH)]
        R_const = reg("const")
        D_const = S.dsem("const")
        bank_ctr = [0]

        held = set()

        def next_bank(hold=False):
            while True:
                b = bank_ctr[0] % 8
                bank_ctr[0] += 1
                if b not in held:
                    break
            if hold:
                held.add(b)
            return b

        wctr = [0]

        def wload(parts):
            s = wctr[0] % 3
            wctr[0] += 1
            for i, (dst, src) in enumerate(parts):
                S.add("pool", i_dma(dst(WR[s]), src), writes=[R_w[s]], dsem=D_w[s], join=(i > 0))
            return s

        def wv(s, nk, ncol):
            return WR[s][:, 0:nk * ncol].rearrange("p (k m) -> p k m", k=nk)

        def cload(dst, src):
            S.add("sp", i_dma(dst, src), writes=[R_const], dsem=D_const, join=True)

        cload(ident_f[:], ident_d)
        cload(rm_f[:], rm_d)
        cload(gains[:], gains_d)
        cload(esink[:], sink_d)
        cload(cw[:], cw_d)
        cload(cb[:], cb_d)
        cload(uflag[:], uflag_d)
        S.barrier()
        S.add("dve", i_vcopy(ident[:], ident_f[:]), reads=[R_const], writes=[R_const])
        S.add("dve", i_vcopy(rm_b[:], rm_f[:]), writes=[R_const])
        S.add("dve", lambda e: e.memset(ones_f[:], 1.0 / 128.0), writes=[R_const])
        S.add("dve", lambda e: e.memset(ones_b[:], 1.0), writes=[R_const])
        S.add("dve", lambda e: e.memset(ones_ms[:], 1.0 / 128.0), writes=[R_const])
        S.add("dve", lambda e: e.memset(epsc[:], EPS), writes=[R_const])
        S.add("act", i_act(esink[:], esink[:], AF.Exp), reads=[R_const], writes=[R_const])
        S.barrier()

        qcols_h = [(HL, HL + 512), (HL + 512, HL + 1024)]

        def hq(kc, n):
            if n < 2:
                return hT3[:, kc, qcols_h[n][0]:qcols_h[n][1]]
            return hT3[:, kc, HL - 1:HL + T + 1:T + 1]

        def hq_regs(n):
            if n == 0:
                return R_hT[3:7]
            if n == 1:
                return R_hT[7:11]
            return [R_hT[2], R_hT[11]]

        QN = [512, 512, 2]
        QO = [(0, 512), (512, 1024), (1024, 1026)]
        KVN = [(0, 512), (512, 1024), (1024, 1536), (1536, 1664)]

        def kv_regs(n):
            lo, hi = KVN[n]
            return R_hT[lo // 128:(hi + 127) // 128]

        def tcol(ap, n):
            if n < 2:
                return ap[:, qcols_h[n][0]:qcols_h[n][1]]
            return ap[:, HL - 1:HL + T + 1:T + 1]

        final_ops = []

        def norm_stage1(slot, xt, xn, junk, gbt, rows, pfx="xt", gbreg="gb"):
            r_xt, r_xn = reg(f"{pfx}{slot}"), reg(f"{pfx}n{slot}")
            r_ss, r_rs = reg(f"ss{slot}"), reg(f"rs{slot}")
            jk = junk if junk is not None else xn[slot]
            jr = reg("junk") if junk is not None else r_xn
            S.add("act", i_act(jk[0:rows, :], xt[slot][0:rows, :], AF.Square,
                               accum_out=ssb[0:rows, slot:slot + 1]),
                  reads=[r_xt], writes=[jr, r_ss])
            S.add("act", i_act(rstdb[0:rows, slot:slot + 1], ssb[0:rows, slot:slot + 1], AF.Ln,
                               scale=1.0 / D, bias=epsc[0:rows, 0:1]), reads=[r_ss, R_const], writes=[r_rs])
            S.add("act", i_act(rstdb[0:rows, slot:slot + 1], rstdb[0:rows, slot:slot + 1], AF.Exp, scale=-0.5),
                  reads=[r_rs], writes=[r_rs])
            S.add("dve", i_stt(xn[slot][0:rows, :], xt[slot][0:rows, :], rstdb[0:rows, slot:slot + 1],
                               gbt[0:rows, :], ALU.mult, ALU.mult),
                  reads=[r_xt, r_rs, reg(gbreg)], writes=[r_xn])

        def norm_stage2(slot, xn, rows, dst_fn, dst_regs, ei, pfx="xt"):
            r_xn = reg(f"{pfx}n{slot}")
            for half in range(2):
                b = next_bank()
                for k in range(8):
                    kk = half * 8 + k
                    S.add("pe", i_tr(psb[:, b, k * 128:k * 128 + rows], xn[slot][0:rows, kk * 128:(kk + 1) * 128],
                                     ident[0:rows, 0:rows]),
                          reads=[r_xn, R_const], writes=[R_bank[b]])
                src = psb[:, b, 0:1024].rearrange("p (k t) -> p k t", k=8)[:, :, 0:rows]
                eng = "act" if (ei + half) % 2 == 0 else "dve"
                if eng == "act":
                    S.add("act", i_act(dst_fn(half), src, AF.Copy), reads=[R_bank[b]], writes=dst_regs)
                else:
                    S.add("dve", i_vcopy(dst_fn(half), src), reads=[R_bank[b]], writes=dst_regs)


        def wsrc(w, c0, n):
            return w[:, c0:c0 + n].rearrange("(k p) m -> p k m", p=128)

        def wload2(nk, ncol, parts):
            s_ = wctr[0] % 3
            wctr[0] += 1
            w3 = wv(s_, nk, ncol)
            for i, (k0, k1, c0, src) in enumerate(parts):
                n = src.shape[-1]
                S.add("pool", i_dma(w3[:, k0:k1, c0:c0 + n], src), writes=[R_w[s_]], dsem=D_w[s_], join=(i > 0))
            return s_

        def gemm_fm(ws, nk, wcols, coff, k0, rhs_fn, rhs_regs_fn, nlist, consume):
            w3 = wv(ws, nk, wcols)
            nkk = nlist[0][2] if len(nlist[0]) > 2 else None
            for (n, nn) in nlist:
                b = next_bank()
                cnt = 16
                for kc in range(cnt):
                    S.add("pe", i_mm(ps[:, b, 0:nn], w3[:, k0 + kc, coff:coff + 128], rhs_fn(kc, n), kc == 0, kc == cnt - 1),
                          reads=[R_w[ws]] + rhs_regs_fn(n), writes=[R_bank[b]])
                consume(n, b)

        def mk_tmp(offs, pfx):
            d = {}
            for nm, (o, n, dt) in offs.items():
                d[nm] = carve(o, n, dt)
                d["r_" + nm] = reg(pfx + nm)
            return d

        class Pipe:
            def __init__(self):
                self.live = []

            def push(self, gen):
                try:
                    next(gen)
                    self.live.append(gen)
                except StopIteration:
                    pass

            def tick(self):
                nl = []
                for g_ in self.live:
                    try:
                        next(g_)
                        nl.append(g_)
                    except StopIteration:
                        pass
                self.live = nl

            def flush(self):
                while self.live:
                    self.tick()

        pipe = Pipe()
        pending_units = []
        unit_ctr = [0]

        round_ctr = [0]
        deferred = []

        def defer_units(delay, items):
            deferred.append((round_ctr[0] + delay, items))

        def round_end(new_gen=None, nunits=0, tmps=None):
            pipe.tick()
            round_ctr[0] += 1
            while deferred and deferred[0][0] <= round_ctr[0]:
                pending_units.extend(deferred.pop(0)[1])
            if new_gen is not None:
                pipe.push(new_gen)
            k = 0
            while k < nunits and pending_units:
                u = pending_units.pop(0)
                if isinstance(u, tuple):
                    u[1]()
                    continue
                t = tmps[unit_ctr[0] % len(tmps)]
                unit_ctr[0] += 1
                pipe.push(unit_gen(u, t))
                k += 1

        def drain_units(nunits, tmps):
            while deferred or pending_units:
                round_end(None, nunits=nunits, tmps=tmps)
            pipe.flush()

        def gemm_tile(ws, nk, wcols, coff, rhs_fn, rhs_regs, nn):
            w3 = wv(ws, nk, wcols)
            b = next_bank(hold=True)
            for kc in range(16):
                S.add("pe", i_mm(ps[:, b, 0:nn], w3[:, kc, coff:coff + 128], rhs_fn(kc), kc == 0, kc == 15),
                      reads=[R_w[ws]] + rhs_regs, writes=[R_bank[b]])
            return b

        def qknorm_gen(b, n, gain_ap, out_ap, out_regs, tmp, rope=None):
            psv = ps[:, b, 0:n]
            sqb = tmp["sq"].bitcast(BF16)
            S.add("act", i_act(sqb[:, 0:n], psv, AF.Square), reads=[R_bank[b]], writes=[tmp["r_sq"]])
            yield
            b2 = next_bank()
            S.add("pe", i_mm(ps[:, b2, 0:n], ones_ms[:, :], sqb[:, 0:n], True, True),
                  reads=[tmp["r_sq"], R_const], writes=[R_bank[b2]])
            S.add("act", i_act(tmp["rstd"][:, 0:n], ps[:, b2, 0:n], AF.Ln, bias=epsc[:, 0:1]),
                  reads=[R_bank[b2], R_const], writes=[tmp["r_rstd"]])
            S.add("act", i_act(tmp["rstd"][:, 0:n], tmp["rstd"][:, 0:n], AF.Exp, scale=-0.5),
                  reads=[tmp["r_rstd"]], writes=[tmp["r_rstd"]])
            if rope is None:
                S.add("dve", i_stt(out_ap, psv, gain_ap, tmp["rstd"][:, 0:n], ALU.mult, ALU.mult),
                      reads=[R_bank[b], tmp["r_rstd"], R_const], writes=out_regs)
                held.discard(b)
                return
            cos_ap, sin_ap = rope
            y32, t1, t2 = tmp["y32"], tmp["t1"], tmp["t2"]
            S.add("dve", i_stt(y32[:, 0:n], psv, gain_ap, tmp["rstd"][:, 0:n], ALU.mult, ALU.mult),
                  reads=[R_bank[b], tmp["r_rstd"], R_const], writes=[tmp["r_y32"]])
            held.discard(b)
            S.add("pool", i_tt(t1[:, 0:n], y32[:, 0:n], cos_ap, ALU.mult),
                  reads=[tmp["r_y32"], reg("rope")], writes=[tmp["r_t1"]])
            S.add("dve", i_tt(t2[0:64, 0:n], y32[64:128, 0:n], sin_ap[64:128], ALU.mult),
                  reads=[tmp["r_y32"], reg("rope")], writes=[tmp["r_t2"]])
            S.add("dve", i_tt(t2[64:128, 0:n], y32[0:64, 0:n], sin_ap[0:64], ALU.mult),
                  reads=[tmp["r_y32"], reg("rope")], writes=[tmp["r_t2"]])
            S.add("pool", i_tt(out_ap, t1[:, 0:n], t2[:, 0:n], ALU.add),
                  reads=[tmp["r_t1"], tmp["r_t2"]], writes=out_regs)

        def vtr_gen(vT, r_vT, Vdst, r_V):
            yield
            for gi, (c0, c1) in enumerate(((0, 8), (8, 13))):
                b = next_bank()
                for c in range(c0, c1):
                    S.add("pe", i_tr(psb[:, b, (c - c0) * 128:(c - c0 + 1) * 128], vT[:, c * 128:(c + 1) * 128], ident[:, :]),
                          reads=[r_vT, R_const], writes=[R_bank[b]])
                src = psb[:, b, 0:(c1 - c0) * 128].rearrange("p (c d) -> p c d", c=c1 - c0)
                if gi == 0:
                    S.add("dve", i_vcopy(Vdst[:, c0:c1, :], src), reads=[R_bank[b]], writes=[r_V])
                    yield
                else:
                    S.add("act", i_act(Vdst[:, c0:c1, :], src, AF.Copy), reads=[R_bank[b]], writes=[r_V])

        def unit_gen(u, t):
            attn_s1(u, t)
            yield
            attn_s2(u, t)

        def attn_s1(u, t):
            nch, nq = len(u["kch"]), u["nq"]
            P = t["P"].rearrange("p (j q) -> p j q", q=128)
            if u.get("mask_pe") is not None:
                b = next_bank()
                for j in range(nch):
                    S.add("pe", i_mm(ps[:, b, j * nq:(j + 1) * nq], u["kch"][j], u["q"], True, False),
                          reads=u["regs_qk"], writes=[R_bank[b]])
                    S.add("pe", i_mm(ps[:, b, j * nq:(j + 1) * nq], ident[:, :], u["mask_pe"][j], False, True),
                          reads=u["regs_tab"] + [R_const], writes=[R_bank[b]])
                src = ps[:, b, 0:nch * nq].rearrange("p (j q) -> p j q", j=nch)
                S.add("act", i_act(P[:, 0:nch, 0:nq], src, AF.Exp, scale=SCALE), reads=[R_bank[b]], writes=[t["r_P"]])
                return
            E = t["E"].rearrange("p (j q) -> p j q", q=128)
            for g0 in range(0, nch, 4):
                g1 = min(nch, g0 + 4)
                b = next_bank()
                for j in range(g0, g1):
                    S.add("pe", i_mm(ps[:, b, (j - g0) * nq:(j - g0 + 1) * nq], u["kch"][j], u["q"], True, True),
                          reads=u["regs_qk"], writes=[R_bank[b]])
                src = ps[:, b, 0:(g1 - g0) * nq].rearrange("p (j q) -> p j q", j=g1 - g0)
                S.add("dve", i_stt(E[:, g0:g1, 0:nq], src, SCALE, u["tab"][:, g0:g1, :], ALU.mult, ALU.add),
                      reads=[R_bank[b]] + u["regs_tab"], writes=[t["r_E"]])
            S.add("act", i_act(P[:, 0:nch, 0:nq], E[:, 0:nch, 0:nq], AF.Exp), reads=[t["r_E"]], writes=[t["r_P"]])

        def attn_s2(u, t):
            nch, nq = len(u["kch"]), u["nq"]
            P = t["P"].rearrange("p (j q) -> p j q", q=128)
            rz = t["rz"]
            b = next_bank()
            for j in range(nch):
                S.add("pe", i_mm(ps[:, b, 0:nq], u["vch"][j], P[:, j, 0:nq], j == 0, j == nch - 1),
                      reads=[t["r_P"]] + u["regs_v"], writes=[R_bank[b]])
            for j in range(nch):
                S.add("pe", i_mm(ps[:, b, 128:128 + nq], ones_b[:, :], P[:, j, 0:nq], j == 0, j == nch - 1),
                      reads=[t["r_P"], R_const], writes=[R_bank[b]])
            if u["sink"] is not None:
                S.add("dve", i_ts(rz[:, 0:nq], ps[:, b, 128:128 + nq], u["sink"], None, ALU.add),
                      reads=[R_bank[b], R_const], writes=[t["r_rz"]])
                S.add("dve", i_recip(rz[:, 0:nq], rz[:, 0:nq]), reads=[t["r_rz"]], writes=[t["r_rz"]])
            else:
                S.add("dve", i_recip(rz[:, 0:nq], ps[:, b, 128:128 + nq]), reads=[R_bank[b]], writes=[t["r_rz"]])
            S.add("dve", i_tt(u["out"], ps[:, b, 0:nq], rz[:, 0:nq], ALU.mult),
                  reads=[R_bank[b], t["r_rz"]], writes=u["regs_out"])

        def run_units(units, tmps):
            prev = None
            for i, u in enumerate(units):
                attn_s1(u, tmps[i % 2])
                if prev is not None:
                    attn_s2(prev[0], prev[1])
                prev = (u, tmps[i % 2])
            if prev is not None:
                attn_s2(prev[0], prev[1])

        def dump(name, ap, regs, shape, dt):
            if name in debug:
                dbg_out[name] = nc.dram_tensor("dbg_" + name, list(shape), dt, kind="ExternalOutput").ap()
                final_ops.append(S.add("sp", i_dma(dbg_out[name], ap), reads=regs, dsem=S.dsem("dbg_" + name)))

        R_oa, R_ob = reg("oa"), reg("ob")
        R_mg = [reg(f"mg{n}") for n in range(3)]
        R_h2T = [reg(f"h2T{t}") for t in range(9)]
        R_act = reg("act")
        R_x1scr = [[reg(f"x1scr{p}_{t}") for t in range(8)] for p in range(NPASS)]
        D_cos, D_sin, D_mask = S.dsem("cos"), S.dsem("sin"), S.dsem("maskA")
        D_tabB = [S.dsem(f"tabB{k}") for k in range(5)]
        R_tabB = [reg(f"tabB{k}") for k in range(5)]
        D_tabBx = S.dsem("tabBx")
        D_xt2 = [S.dsem(f"xt2_{i}") for i in range(2)]
        D_x1st = [S.dsem(f"x1st_{i}") for i in range(2)]
        D_gb2 = S.dsem("gb2")
        D_oa = S.dsem("oa_w")
        D_x1s = [S.dsem(f"x1s{i}") for i in range(8)]
        D_yst = [S.dsem(f"yst{i}") for i in range(4)]
        D_xtA = [S.dsem(f"xtA{i}") for i in range(2)]
        D_gbA = S.dsem("gbA")

        for p in range(npass):
            xt = [carve(0, 4096, F32), carve(4096, 4096, F32)]
            xn = [carve(8192, 2048), carve(10240, 2048)]
            junk = carve(12288, 2048)
            gbt = carve(14336, 4096, F32)
            D_xt = D_xtA
            D_gb = D_gbA
            S.barrier()
            S.add("sp", i_dma(gbt[:, :], gmix_d), writes=[reg("gb")], dsem=D_gb)
            prev = None
            for c in range(NCH + 1):
                if c < NCH:
                    s = c % 2
                    S.add("sp", i_dma(xt[s][:, :], xe[p, c * 128:(c + 1) * 128, :]),
                          writes=[reg(f"xt{s}")], dsem=D_xt[s])
                    norm_stage1(s, xt, xn, junk, gbt, 128)
                if prev is not None:
                    cc, ss_ = prev
                    norm_stage2(ss_, xn, 128,
                                lambda half, cc=cc: hT3[:, half * 8:(half + 1) * 8, cc * 128:(cc + 1) * 128],
                                [R_hT[cc]], cc)
                prev = (c, c % 2) if c < NCH else None
            if "hT" in debug and p == 0:
                dbg_out["hT"] = nc.dram_tensor("dbg_hT", [128, 16 * NKV], BF16, kind="ExternalOutput").ap()
                final_ops.append(S.add("sp", i_dma(dbg_out["hT"], HT[:, :]), reads=R_hT, dsem=S.dsem("dbg_hT")))
            if stop_after == "A":
                break

            S.barrier()
            cosb = carve(0, 3328, F32)
            sinb = carve(3328, 3328, F32)
            maskA = carve(6656, 2304, F32).rearrange("p (a j q) -> p a j q", a=3, j=3)
            qA = carve(8960, 4104).rearrange("p (h t) -> p h t", h=4)
            kTa = carve(13064, 1664)
            vTa = carve(14728, 1664)
            Va = carve(16392, 1664).rearrange("p (c d) -> p c d", c=13)
            tmpA = [mk_tmp(dict(sq=(18056, 1024, F32), rstd=(19080, 1024, F32), y32=(20104, 1024, F32),
                                t1=(21128, 1024, F32), t2=(22152, 1024, F32)), "tA0"),
                    mk_tmp(dict(sq=(23176, 1024, F32), rstd=(24200, 1024, F32), y32=(25224, 1024, F32),
                                t1=(26248, 1024, F32), t2=(27272, 1024, F32)), "tA1")]
            atA = [mk_tmp(dict(P=(28296 + i * 640, 384, BF16), rz=(28296 + i * 640 + 384, 256, F32)), f"aA{i}")
                   for i in range(4)]
            maskbf = carve(30856, 1152).rearrange("p (a j q) -> p a j q", a=3, j=3)
            r_qA = [reg(f"qA{i}") for i in range(4)]
            r_kTa, r_vTa, r_Va = reg("kTa"), reg("vTa"), reg("Va")
            S.add("sp", i_dma(cosb[:, :], cos_d[p]), writes=[reg("rope")], dsem=D_cos)
            S.add("sp", i_dma(sinb[:, :], sin_d[p]), writes=[reg("rope")], dsem=D_sin, join=True)
            S.add("sp", i_dma(carve(6656, 2304, F32), maskA_d[p]), writes=[reg("maskA32")], dsem=D_mask)
            S.add("dve", i_vcopy(carve(30856, 1152), carve(6656, 2304, F32)), reads=[reg("maskA32")], writes=[reg("maskA")])
            tq = [0]

            def nxt_tmpA():
                tq[0] += 1
                return tmpA[tq[0] % 2]

            kv_list = [(n, KVN[n][1] - KVN[n][0]) for n in range(4)]
            q_list = [(0, 512), (1, 512), (2, 2)]
            def load_kv(g_):
                return wload2(16, 256, [(0, 16, 0, wsrc(w_in, 1024 + g_ * 128, 128)),
                                        (0, 16, 128, wsrc(w_in, 1280 + g_ * 128, 128))])

            ws_kv_next = load_kv(0)
            for g in range(2):
                ws = ws_kv_next
                ws2 = wload2(16, 512, [(0, 16, 0, wsrc(w_in, g * 512, 512))])
                for (n, nn) in kv_list:
                    lo, hi = KVN[n]
                    b = gemm_tile(ws, 16, 256, 0, lambda kc, lo=lo, hi=hi: hT3[:, kc, lo:hi], kv_regs(n), nn)
                    round_end(qknorm_gen(b, nn, gains[:, 1:2], kTa[:, lo:hi], [r_kTa], nxt_tmpA(),
                                         rope=(cosb[:, lo:hi], sinb[:, lo:hi])))
                for (n, nn) in kv_list:
                    lo, hi = KVN[n]
                    b = gemm_tile(ws, 16, 256, 128, lambda kc, lo=lo, hi=hi: hT3[:, kc, lo:hi], kv_regs(n), nn)
                    S.add("act", i_act(vTa[:, lo:hi], ps[:, b, 0:nn], AF.Copy), reads=[R_bank[b]], writes=[r_vTa])
                    held.discard(b)
                    round_end(None)
                pipe.push(vtr_gen(vTa, r_vTa, Va, r_Va))
                if g == 0:
                    ws_kv_next = load_kv(1)
                for hh in range(4):
                    for (n, nn) in q_list:
                        b = gemm_tile(ws2, 16, 512, hh * 128, lambda kc, n=n: hq(kc, n), hq_regs(n), nn)
                        round_end(qknorm_gen(b, nn, gains[:, 0:1], qA[:, hh, QO[n][0]:QO[n][1]], [r_qA[hh]], nxt_tmpA(),
                                             rope=(tcol(cosb, n), tcol(sinb, n))), nunits=4, tmps=atA)
                    hq_ = g * 4 + hh
                    batch = []
                    for i in range(8):
                        c = 3 + i
                        kind = 0 if i == 0 else (2 if i == 7 else 1)
                        batch.append(dict(q=qA[:, hh, i * 128:(i + 1) * 128], nq=128,
                                                  kch=[kTa[:, cc * 128:(cc + 1) * 128] for cc in (c - 1, c, c + 1)],
                                                  vch=[Va[:, cc, :] for cc in (c - 1, c, c + 1)],
                                                  tab=None, mask_pe=[maskbf[:, kind, j, :] for j in range(3)],
                                                  sink=esink[:, hq_:hq_ + 1],
                                                  out=out_aT[:, hq_, i * 128:(i + 1) * 128],
                                                  regs_qk=[r_qA[hh], r_kTa], regs_v=[r_Va], regs_tab=[reg("maskA")],
                                                  regs_out=[R_oa]))
                    for (col, chunks, qi) in ((1024, (1, 2, 3), 127), (1025, (10, 11, 12), 0)):
                        batch.append(dict(q=qA[:, hh, col:col + 1], nq=1,
                                                  kch=[kTa[:, cc * 128:(cc + 1) * 128] for cc in chunks],
                                                  vch=[Va[:, cc, :] for cc in chunks],
                                                  tab=None, mask_pe=[maskbf[:, 1, j, qi:qi + 1] for j in range(3)],
                                                  sink=esink[:, hq_:hq_ + 1],
                                                  out=out_aT[:, hq_, col:col + 1],
                                                  regs_qk=[r_qA[hh], r_kTa], regs_v=[r_Va], regs_tab=[reg("maskA")],
                                                  regs_out=[R_oa]))
                    defer_units(2, batch)
                drain_units(2, atA)
                if g == 0 and p == 0:
                    dump("kTa", kTa[:, :], [r_kTa], [128, 1664], BF16)
                    dump("qA", carve(8960, 4104), r_qA, [128, 4104], BF16)
                    dump("Va", carve(16392, 1664), [r_Va], [128, 1664], BF16)
            if p == 0:
                dump("oa", BIG[:, OA0:OA0 + 8208], [R_oa], [128, 8208], BF16)
            if stop_after == "B":
                break

            S.barrier()
            tabB = carve(0, 6912, F32).rearrange("p (j q) -> p j q", j=27)
            qTb = [carve(6912, 1026), carve(7938, 1026)]
            kTb = [carve(8964, 1664), carve(10628, 1664)]
            vTb = carve(12292, 1664)
            Vb = [carve(13956, 1664).rearrange("p (c d) -> p c d", c=13),
                  carve(15620, 1664).rearrange("p (c d) -> p c d", c=13)]
            tmpB = [mk_tmp(dict(sq=(17284, 1024, F32), rstd=(18308, 1024, F32)), "tB0"),
                    mk_tmp(dict(sq=(19332, 1024, F32), rstd=(20356, 1024, F32)), "tB1")]
            atB = [mk_tmp(dict(E=(21380 + i * 2560, 1536, F32), P=(21380 + i * 2560 + 1536, 768, BF16),
                               rz=(21380 + i * 2560 + 2304, 256, F32)), f"aB{i}") for i in range(4)]
            S.add("sp", i_dma(tabBx[:, :], tabBx_d[p]), writes=[reg("tabBx")], dsem=D_tabBx)
            tqb = [0]

            def nxt_tmpB():
                tqb[0] += 1
                return tmpB[tqb[0] % 2]

            for h in range(8):
                s_ = h % 2
                r_q, r_k, r_vT, r_V = reg(f"qTb{s_}"), reg(f"kTb{s_}"), reg("vTb"), reg(f"Vb{s_}")
                ws = wload2(16, 384, [(0, 16, 0, wsrc(w_in, 1536 + h * 128, 128)),
                                      (0, 16, 128, wsrc(w_in, 2560 + h * 128, 128)),
                                      (0, 16, 256, wsrc(w_in, 3584 + h * 128, 128))])
                for (n, nn) in kv_list:
                    lo, hi = KVN[n]
                    b = gemm_tile(ws, 16, 384, 128, lambda kc, lo=lo, hi=hi: hT3[:, kc, lo:hi], kv_regs(n), nn)
                    round_end(qknorm_gen(b, nn, gains[:, 3:4], kTb[s_][:, lo:hi], [r_k], nxt_tmpB()), nunits=2, tmps=atB)
                for (n, nn) in kv_list:
                    lo, hi = KVN[n]
                    b = gemm_tile(ws, 16, 384, 256, lambda kc, lo=lo, hi=hi: hT3[:, kc, lo:hi], kv_regs(n), nn)
                    S.add("act", i_act(vTb[:, lo:hi], ps[:, b, 0:nn], AF.Copy), reads=[R_bank[b]], writes=[r_vT])
                    held.discard(b)
                    round_end(None, nunits=2, tmps=atB)
                pipe.push(vtr_gen(vTb, r_vT, Vb[s_], r_V))
                for (n, nn) in q_list:
                    b = gemm_tile(ws, 16, 384, 0, lambda kc, n=n: hq(kc, n), hq_regs(n), nn)
                    round_end(qknorm_gen(b, nn, gains[:, 2:3], qTb[s_][:, QO[n][0]:QO[n][1]], [r_q], nxt_tmpB()),
                              nunits=2, tmps=atB)
                assert not pending_units and not deferred, "units of the previous head must be registered by now"

                def load_tabs(h=h):
                    for kind in range(5):
                        off = B_KOFF[kind]
                        nchk = len(B_KINDS[kind][1])
                        S.add("sp", i_dma(tabB[:, off:off + nchk, :],
                                          tabB_d[p, h, :, off * 128:(off + nchk) * 128].rearrange("p (j q) -> p j q", q=128)),
                              writes=[R_tabB[kind]], dsem=D_tabB[kind])
                batch = [("call", load_tabs)]
                for i in range(8):
                    kind = 0 if i == 0 else 1 if i == 1 else 3 if i == 6 else 4 if i == 7 else 2
                    chunks = B_KINDS[kind][1] if kind != 2 else list(range(i + 1, i + 6))
                    off = B_KOFF[kind]
                    batch.append(dict(q=qTb[s_][:, i * 128:(i + 1) * 128], nq=128,
                                              kch=[kTb[s_][:, cc * 128:(cc + 1) * 128] for cc in chunks],
                                              vch=[Vb[s_][:, cc, :] for cc in chunks],
                                              tab=tabB[:, off:off + len(chunks), :], sink=None,
                                              out=out_bT[:, h, i * 128:(i + 1) * 128],
                                              regs_qk=[r_q, r_k], regs_v=[r_V], regs_tab=[R_tabB[kind]], regs_out=[R_ob]))
                for (col, chunks, toff) in ((1024, list(range(0, 5)), 0), (1025, list(range(9, 13)), 5)):
                    batch.append(dict(q=qTb[s_][:, col:col + 1], nq=1,
                                              kch=[kTb[s_][:, cc * 128:(cc + 1) * 128] for cc in chunks],
                                              vch=[Vb[s_][:, cc, :] for cc in chunks],
                                              tab=tabBx[:, h * 9 + toff:h * 9 + toff + len(chunks)].rearrange("p (j q) -> p j q", q=1),
                                              sink=None, out=out_bT[:, h, col:col + 1],
                                              regs_qk=[r_q, r_k], regs_v=[r_V], regs_tab=[reg("tabBx")], regs_out=[R_ob]))
                defer_units(1, batch)
            drain_units(2, atB)
            if p == 0:
                dump("ob", BIG[:, OB0:OB0 + 8208], [R_ob], [128, 8208], BF16)
            if stop_after == "C":
                break

            S.barrier()
            mg = carve(0, 16416).rearrange("p (k t) -> p k t", k=16)
            sga = [carve(16416, 1024, F32), carve(17440, 1024, F32)]
            sgb = [carve(18464, 1024, F32), carve(19488, 1024, F32)]
            m1 = [carve(20512, 1024, F32), carve(21536, 1024, F32)]
            m2 = [carve(22560, 1024, F32), carve(23584, 1024, F32)]
            for j in range(16):
                ws = wload2(48, 128, [(0, 16, 0, wsrc(w_in, 4608 + j * 128, 128)),
                                      (16, 32, 0, wsrc(w_in, 6656 + j * 128, 128)),
                                      (32, 40, 0, wsrc(w_ba, j * 128, 128)),
                                      (40, 48, 0, wsrc(w_bb, j * 128, 128))])
                w3 = wv(ws, 48, 128)
                for n in range(3):
                    nn = QN[n]
                    o0, o1 = QO[n]
                    t = (j * 3 + n) % 2
                    r_sga, r_sgb, r_m1, r_m2 = reg(f"sga{t}"), reg(f"sgb{t}"), reg(f"m1{t}"), reg(f"m2{t}")
                    b1 = next_bank()
                    for kc in range(16):
                        S.add("pe", i_mm(ps[:, b1, 0:nn], w3[:, kc, :], hq(kc, n), kc == 0, kc == 15),
                              reads=[R_w[ws]] + hq_regs(n), writes=[R_bank[b1]])
                    S.add("act", i_act(sga[t][:, 0:nn], ps[:, b1, 0:nn], AF.Sigmoid), reads=[R_bank[b1]], writes=[r_sga])
                    b2 = next_bank()
                    for kc in range(16):
                        S.add("pe", i_mm(ps[:, b2, 0:nn], w3[:, 16 + kc, :], hq(kc, n), kc == 0, kc == 15),
                              reads=[R_w[ws]] + hq_regs(n), writes=[R_bank[b2]])
                    S.add("act", i_act(sgb[t][:, 0:nn], ps[:, b2, 0:nn], AF.Sigmoid), reads=[R_bank[b2]], writes=[r_sgb])
                    b3 = next_bank()
                    for kc in range(8):
                        S.add("pe", i_mm(ps[:, b3, 0:nn], w3[:, 32 + kc, :], out_aT[:, kc, o0:o1], kc == 0, kc == 7),
                              reads=[R_w[ws], R_oa], writes=[R_bank[b3]])
                    S.add("dve", i_tt(m1[t][:, 0:nn], ps[:, b3, 0:nn], sga[t][:, 0:nn], ALU.mult),
                          reads=[R_bank[b3], r_sga], writes=[r_m1])
                    b4 = next_bank()
                    for kc in range(8):
                        S.add("pe", i_mm(ps[:, b4, 0:nn], w3[:, 40 + kc, :], out_bT[:, kc, o0:o1], kc == 0, kc == 7),
                              reads=[R_w[ws], R_ob], writes=[R_bank[b4]])
                    S.add("dve", i_tt(m2[t][:, 0:nn], ps[:, b4, 0:nn], sgb[t][:, 0:nn], ALU.mult),
                          reads=[R_bank[b4], r_sgb], writes=[r_m2])
                    S.add("dve", i_tt(mg[:, j, o0:o1], m1[t][:, 0:nn], m2[t][:, 0:nn], ALU.add),
                          reads=[r_m1, r_m2], writes=[R_mg[n]])
            if p == 0:
                dump("mg", carve(0, 16416), R_mg, [128, 16416], BF16)
            if stop_after == "D":
                break

            S.barrier()
            xt2 = [carve(16416, 4096, F32), carve(20512, 4096, F32)]
            x1t = [carve(24608, 4096, F32), carve(28704, 4096, F32)]
            xn2 = [BIG[:, OB0:OB0 + 2048], BIG[:, OB0 + 2048:OB0 + 4096]]
            gb2 = BIG[:, OB0 + 4096:OB0 + 8192].bitcast(F32)
            h2T = HT[:, 0:16416].rearrange("p (k t) -> p k t", k=16)
            S.add("sp", i_dma(gb2[:, :], gffn_d), writes=[reg("gb2")], dsem=D_gb2)
            wo = []
            for fg in range(3):
                ws = wload2(16, 512, [(0, 16, 0, wsrc(w_out, fg * 512, 512))])
                wo.append((wv(ws, 16, 512), R_w[ws]))
            w4 = BIG[:, OA0:OA0 + 8192].rearrange("p (k m) -> p k m", k=16)
            S.add("pool", i_dma(w4, wsrc(w_out, 1536, 512)), writes=[R_oa], dsem=D_oa)
            wo.append((w4, R_oa))
            prev = None
            for tt in range(10):
                if tt < 9:
                    s_ = tt % 2
                    rows = 128 if tt < 8 else 2
                    cols = (tt * 128, tt * 128 + 128) if tt < 8 else (1024, 1026)
                    r_xt2, r_x1t = reg(f"xt2_{s_}"), reg(f"x1t{s_}")
                    if tt < 8:
                        S.add("sp", i_dma(xt2[s_][:, :], xe[p, HL + tt * 128:HL + (tt + 1) * 128, :]),
                              writes=[r_xt2], dsem=D_xt2[s_])
                    else:
                        S.add("sp", i_dma(xt2[s_][0:1, :], xe[p, HL - 1:HL, :]), writes=[r_xt2], dsem=D_xt2[s_])
                        S.add("sp", i_dma(xt2[s_][1:2, :], xe[p, HL + T:HL + T + 1, :]), writes=[r_xt2],
                              dsem=D_xt2[s_], join=True)
                    for fg in range(4):
                        b = next_bank()
                        w3, rw = wo[fg]
                        for kc in range(16):
                            S.add("pe", i_mm(ps[0:rows, b, 0:512], mg[:, kc, cols[0]:cols[1]], w3[:, kc, :], kc == 0, kc == 15),
                                  reads=[rw, R_mg[min(tt // 4, 2)]], writes=[R_bank[b]])
                        S.add("dve", i_tt(x1t[s_][0:rows, fg * 512:(fg + 1) * 512], ps[0:rows, b, 0:512],
                                          xt2[s_][0:rows, fg * 512:(fg + 1) * 512], ALU.add),
                              reads=[R_bank[b], r_xt2], writes=[r_x1t])
                    if tt < 8:
                        S.add("sp", i_dma(x1scr[p * T + tt * 128:p * T + (tt + 1) * 128, :], x1t[s_][:, :]),
                              reads=[r_x1t], writes=[R_x1scr[p][tt]], dsem=D_x1st[s_])
                    norm_stage1(s_, x1t, xn2, None, gb2, rows, pfx="x1t", gbreg="gb2")
                if prev is not None:
                    ptt, ps_, prow, pcols = prev
                    norm_stage2(ps_, xn2, prow,
                                lambda half, pcols=pcols: h2T[:, half * 8:(half + 1) * 8, pcols],
                                [R_h2T[ptt]], ptt, pfx="x1t")
                prev = (tt, tt % 2, 128 if tt < 8 else 2,
                        slice(1 + tt * 128, 1 + (tt + 1) * 128) if tt < 8 else slice(0, 1026, 1025)) if tt < 9 else None
            if p == 0:
                dump("h2T", HT[:, 0:16416], R_h2T, [128, 16416], BF16)
            if stop_after == "E":
                break

            S.barrier()
            act_t = BIG[:, 0:45056].rearrange("p (j t) -> p j t", j=44)
            ubg = HT[:, 16416:18468].bitcast(F32)
            ubv = HT[:, 18468:20520].bitcast(F32)
            tg = HT[:, 20520:22568].bitcast(F32)
            tv = HT[:, 22568:24616].bitcast(F32)
            cw3 = cw[:, :].rearrange("p (f k) -> p f k", k=3)
            r_ubg, r_ubv, r_tg, r_tv = reg("ubg"), reg("ubv"), reg("tg"), reg("tv")

            UQ = [(0, 342), (342, 684), (684, 1026)]

            def h2_regs(n):
                return [R_h2T[0:3] + [R_h2T[8]], R_h2T[2:6], R_h2T[5:8] + [R_h2T[8]]][n]

            for u in range(22):
                ws = wload2(16, 512, [(0, 16, 0, wsrc(w_up, 2 * u * 128, 256)),
                                      (0, 16, 256, wsrc(w_up, DFF + 2 * u * 128, 256))])
                for q in range(2):
                    j = 2 * u + q
                    for (coff, ub, r_ub, tt_, r_t, f) in ((q * 128, ubg, r_ubg, tg, r_tg, j),
                                                           (256 + q * 128, ubv, r_ubv, tv, r_tv, NFF + j)):
                        def cons_u(n, b, ub=ub, r_ub=r_ub):
                            S.add("act", i_act(ub[:, UQ[n][0]:UQ[n][1]], ps[:, b, 0:342], AF.Copy),
                                  reads=[R_bank[b]], writes=[r_ub])
                        gemm_fm(ws, 16, 512, coff, 0, lambda kc, n: h2T[:, kc, UQ[n][0]:UQ[n][1]], h2_regs,
                                [(0, 342), (1, 342), (2, 342)], cons_u)
                        S.add("dve", i_tt(ub[:, 0:1026:1025], ub[:, 0:1026:1025], uflag[:, 2 * p:2 * p + 2], ALU.mult),
                              reads=[r_ub, R_const], writes=[r_ub])
                        S.add("dve", i_ts(tt_[:, :], ub[:, 1:1025], cw3[:, f, 1:2], cb[:, f:f + 1], ALU.mult, ALU.add),
                              reads=[r_ub, R_const], writes=[r_t])
                        S.add("dve", i_stt(tt_[:, :], ub[:, 0:1024], cw3[:, f, 0:1], tt_[:, :], ALU.mult, ALU.add),
                              reads=[r_ub, r_t, R_const], writes=[r_t])
                        S.add("dve", i_stt(tt_[:, :], ub[:, 2:1026], cw3[:, f, 2:3], tt_[:, :], ALU.mult, ALU.add),
                              reads=[r_ub, r_t, R_const], writes=[r_t])
                    S.add("act", i_act(tg[:, :], tg[:, :], AF.Silu), reads=[r_tg], writes=[r_tg])
                    S.add("dve", i_tt(act_t[:, j, :], tg[:, :], tv[:, :], ALU.mult), reads=[r_tg, r_tv], writes=[R_act])
            if p == 0:
                dump("act", BIG[:, 0:45056], [R_act], [128, 45056], BF16)
            if stop_after == "F":
                break

            S.barrier()
            x1s = [HT[:, i * 1024:(i + 1) * 1024].bitcast(F32) for i in range(8)]
            yst = [HT[:, 8192 + i * 1024:8192 + (i + 1) * 1024].bitcast(F32) for i in range(4)]
            for fg in range(4):
                for tt in range(8):
                    S.add("sp", i_dma(x1s[tt][:, :], x1scr[p * T + tt * 128:p * T + (tt + 1) * 128, fg * 512:(fg + 1) * 512]),
                          reads=[R_x1scr[p][tt]], writes=[reg(f"x1s{tt}")], dsem=D_x1s[tt])
                for ku in range(4):
                    ws = wload2(11, 512, [(0, 11, 0, w_down[ku * 1408:(ku + 1) * 1408, fg * 512:(fg + 1) * 512]
                                           .rearrange("(k p) m -> p k m", p=128))])
                    w3 = wv(ws, 11, 512)
                    for tt in range(8):
                        for kc in range(11):
                            S.add("pe", i_mm(ps[:, tt, :], act_t[:, ku * 11 + kc, tt * 128:(tt + 1) * 128], w3[:, kc, :],
                                             ku == 0 and kc == 0, ku == 3 and kc == 10),
                                  reads=[R_w[ws], R_act], writes=[R_bank[tt]])
                for tt in range(8):
                    r = (fg * 8 + tt) % 4
                    S.add("dve", i_tt(yst[r][:, :], ps[:, tt, :], x1s[tt][:, :], ALU.add),
                          reads=[R_bank[tt], reg(f"x1s{tt}")], writes=[reg(f"yst{r}")])
                    final_ops.append(S.add("sp", i_dma(out_d[p * T + tt * 128:p * T + (tt + 1) * 128, fg * 512:(fg + 1) * 512],
                                                       yst[r][:, :]),
                                           reads=[reg(f"yst{r}")], dsem=D_yst[r]))

        S.barrier()
        S.final_wait("sp", final_ops)
        S.emit()
    return nc, dbg_out


def _rope_tables(t0):
    pos = (t0 - HL + np.arange(NKV)).astype(np.float32)
    inv_freq = (np.float32(10000.0) ** (-np.arange(64, dtype=np.float32) * np.float32(2.0 / 128))).astype(np.float32)
    ang = pos[:, None] * inv_freq[None, :]
    cos = np.cos(ang).astype(np.float32)
    sin = np.sin(ang).astype(np.float32)
    cosT = np.concatenate([cos, cos], axis=1).T.copy()
    sinT = np.concatenate([sin, -sin], axis=1).T.copy()
    return cosT, sinT


def _maskA(t0):
    out = np.full((128, 3, 3, 128), NEG, np.float32)
    k = np.arange(128)[:, None]
    q = np.arange(128)[None, :]
    for kind, i in enumerate((0, 3, 7)):
        c = 3 + i
        for j in range(3):
            kpos = t0 - HL + (c - 1 + j) * 128 + k
            qpos = t0 + i * 128 + q
            valid = (np.abs(qpos - kpos) <= 128) & (kpos >= 0) & (kpos < S_LEN)
            out[:, kind, j, :] = np.where(valid, 0.0, NEG / SCALE)
    return out.reshape(128, 9 * 128)


def _b_tab(rpb_h, qr, qc, kr, kc):
    qr_, qc_ = qr[None, :], qc[None, :]
    kr_, kc_ = kr[:, None], kc[:, None]
    rs = np.clip(qr_ - 4, 0, 64 - 8)
    cs = np.clip(qc_ - 8, 0, 64 - 16)
    valid = (kr_ >= rs) & (kr_ < rs + 8) & (kc_ >= cs) & (kc_ < cs + 16) & (kr_ >= 0) & (kr_ < 64)
    ir = np.clip(kr_ - qr_ + 7, 0, 14)
    ic = np.clip(kc_ - qc_ + 15, 0, 30)
    return np.where(valid, rpb_h[ir, ic], np.float32(NEG)).astype(np.float32)


B_KINDS = [(0, list(range(1, 7))), (1, list(range(2, 7))), (2, list(range(3, 8))),
           (6, list(range(7, 12))), (7, list(range(7, 13)))]
B_KOFF = [0, 6, 11, 16, 21]


def _tabB(rpb, t0):
    r0 = t0 // 64
    tab = np.empty((8, 128, 27, 128), np.float32)
    kk = np.arange(128)
    for h in range(8):
        for (i, chunks), off in zip(B_KINDS, B_KOFF):
            qr = r0 + 2 * i + kk // 64
            qc = kk % 64
            for j, cc in enumerate(chunks):
                kr = r0 - 6 + 2 * cc + kk // 64
                kc = kk % 64
                tab[h, :, off + j, :] = _b_tab(rpb[h], qr, qc, kr, kc)
    tabx = np.empty((128, 8, 9), np.float32)
    for h in range(8):
        for j, cc in enumerate(range(0, 5)):
            kr = r0 - 6 + 2 * cc + kk // 64
            tabx[:, h, j] = _b_tab(rpb[h], np.array([r0 - 1]), np.array([63]), kr, kk % 64)[:, 0]
        for j, cc in enumerate(range(9, 13)):
            kr = r0 - 6 + 2 * cc + kk // 64
            tabx[:, h, 5 + j] = _b_tab(rpb[h], np.array([r0 + 16]), np.array([0]), kr, kk % 64)[:, 0]
    return tab.reshape(8, 128, 27 * 128), tabx.reshape(128, 72)


def host_prepare(inputs, cores=range(8)):
    f = lambda a: np.ascontiguousarray(np.asarray(a, dtype=np.float32))
    x = f(inputs["x"])
    shared = {
        "w_in": f(inputs["w_in"][0]), "w_ba": f(inputs["w_branch_a"][0]), "w_bb": f(inputs["w_branch_b"][0]),
        "w_out": f(inputs["w_out"][0]), "w_up": f(inputs["w_up"][0]), "w_down": f(inputs["w_down"][0]),
        "gmix_b": np.ascontiguousarray(np.broadcast_to(f(inputs["norm_mix"][0])[None, :], (128, D))),
        "gffn_b": np.ascontiguousarray(np.broadcast_to(f(inputs["norm_ffn"][0])[None, :], (128, D))),
        "gains": np.ascontiguousarray(np.stack([f(inputs["a_q_norm"][0]), f(inputs["a_k_norm"][0]),
                                                f(inputs["b_q_norm"][0]), f(inputs["b_k_norm"][0])], axis=1)),
        "sink_b": np.ascontiguousarray(np.broadcast_to(f(inputs["a_sink"][0])[None, :], (128, 8))),
        "cw_fm": np.ascontiguousarray(f(inputs["conv_w"][0]).T.reshape(88, 128, 3).transpose(1, 0, 2).reshape(128, 264)),
        "cb_fm": np.ascontiguousarray(f(inputs["conv_b"][0]).reshape(88, 128).T),
        "ident": np.eye(128, dtype=np.float32),
    }
    rm = np.zeros((128, 128), np.float32)
    for m in range(64):
        rm[m + 64, m] = -1.0
        rm[m, m + 64] = 1.0
    shared["rm"] = rm
    rpb = f(inputs["b_rpb"][0])
    per_t0 = {}
    for t0 in range(0, S_LEN, T):
        cosT, sinT = _rope_tables(t0)
        tb, tbx = _tabB(rpb, t0)
        per_t0[t0] = (cosT, sinT, _maskA(t0), tb, tbx)
    in_maps = []
    for c in cores:
        b, half = c // 2, c % 2
        m = dict(shared)
        xes, cs, sn, ma, tb, tbx = [], [], [], [], [], []
        for p in range(NPASS):
            t0 = half * TOK_CORE + p * T
            lo, hi = t0 - HL, t0 - HL + NKV
            xp = np.zeros((NKV, D), np.float32)
            a, bb = max(lo, 0), min(hi, S_LEN)
            xp[a - lo:bb - lo] = x[b, a:bb]
            xes.append(xp)
            t = per_t0[t0]
            cs.append(t[0]); sn.append(t[1]); ma.append(t[2]); tb.append(t[3]); tbx.append(t[4])
        m["xe"] = np.stack(xes)
        m["cos_t"] = np.stack(cs)
        m["sin_t"] = np.stack(sn)
        m["maskA"] = np.stack(ma)
        m["tabB"] = np.stack(tb)
        m["tabBx"] = np.stack(tbx)
        fl = np.zeros((128, 2 * NPASS), np.float32)
        for p in range(NPASS):
            t0 = half * TOK_CORE + p * T
            fl[:, 2 * p] = 1.0 if t0 - 1 >= 0 else 0.0
            fl[:, 2 * p + 1] = 1.0 if t0 + T < S_LEN else 0.0
        m["uflag"] = fl
        in_maps.append(m)
    return in_maps


_CACHE = {}


def kernel(**inputs):
    in_maps = host_prepare(inputs)
    if "nc" not in _CACHE:
        _CACHE["nc"] = build_program()[0]
    nc = _CACHE["nc"]
    res = run_bass_kernel_spmd(nc, in_maps, core_ids=list(range(8)))
    out = np.empty((4, S_LEN, D), np.float32)
    for c in range(8):
        b, half = c // 2, c % 2
        out[b, half * TOK_CORE:(half + 1) * TOK_CORE] = res.results[c]["out"]
    return out
```
